# Optimizing a Trainium2 kernel written in Bass

```python
import math
import jax, jax.numpy as jnp
from jax import lax
import numpy as np

D_MODEL = 1024
BATCH = 8
SEQ = 8192
DEPTH = 1

N_META = 16
MIX_WIDTH = D_MODEL
SSM_WIDTH = MIX_WIDTH // 2
POOL_WIDTH = MIX_WIDTH - SSM_WIDTH
SSM_GROUP_CH = 16
SSM_GROUPS = SSM_WIDTH // SSM_GROUP_CH
SSM_STATE = 64
DT_MIN = 1e-3
DT_MAX = 1e-1
POOL_WINDOWS = (2, 4, 8, 16)
POOL_GROUPS = len(POOL_WINDOWS)
POOL_GROUP = POOL_WIDTH // POOL_GROUPS
D_FF = ((8 * D_MODEL // 3 + 127) // 128) * 128
RMS_EPS = 1e-6

kernel_name = "hymba_s5_poolformer_macaron_layer"


def rms_norm(x, g):
    xf = x.astype(jnp.float32)
    y = xf * lax.rsqrt(jnp.mean(xf * xf, axis=-1, keepdims=True) + RMS_EPS)
    return (y * g.astype(jnp.float32)).astype(x.dtype)


def swiglu(h, w_gate, w_up, w_down):
    return (jax.nn.silu(h @ w_gate) * (h @ w_up)) @ w_down


def _complex_scan_combine(e1, e2):
    a1r, a1i, b1r, b1i = e1
    a2r, a2i, b2r, b2i = e2
    ar = a2r * a1r - a2i * a1i
    ai = a2r * a1i + a2i * a1r
    a2r_b = a2r[:, None]
    a2i_b = a2i[:, None]
    br = a2r_b * b1r - a2i_b * b1i + b2r
    bi = a2r_b * b1i + a2i_b * b1r + b2i
    return (ar, ai, br, bi)


def s5_mixer(u, lam_re, lam_im, log_dt, b_re, b_im, c_re, c_im, d_skip, w_glu):
    Bt, L, _ = u.shape
    ug = u.astype(jnp.float32).reshape(Bt, L, SSM_GROUPS, SSM_GROUP_CH)
    lr = lam_re.astype(jnp.float32)
    li = lam_im.astype(jnp.float32)
    dt = jnp.exp(log_dt.astype(jnp.float32))[:, None]
    decay = jnp.exp(lr * dt)
    ang = li * dt
    a_re = decay * jnp.cos(ang)
    a_im = decay * jnp.sin(ang)
    nr = a_re - 1.0
    den = lr * lr + li * li
    q_re = (nr * lr + a_im * li) / den
    q_im = (a_im * lr - nr * li) / den
    br = b_re.astype(jnp.float32)
    bi = b_im.astype(jnp.float32)
    bb_re = q_re[..., None] * br - q_im[..., None] * bi
    bb_im = q_re[..., None] * bi + q_im[..., None] * br
    bu_re = jnp.einsum('blgh,gnh->lbgn', ug, bb_re)
    bu_im = jnp.einsum('blgh,gnh->lbgn', ug, bb_im)
    A_re = jnp.broadcast_to(a_re, (L,) + a_re.shape)
    A_im = jnp.broadcast_to(a_im, (L,) + a_im.shape)
    _, _, x_re, x_im = lax.associative_scan(_complex_scan_combine, (A_re, A_im, bu_re, bu_im), axis=0)
    y = (jnp.einsum('lbgn,ghn->blgh', x_re, c_re.astype(jnp.float32))
         - jnp.einsum('lbgn,ghn->blgh', x_im, c_im.astype(jnp.float32)))
    y = y + d_skip.astype(jnp.float32).reshape(SSM_GROUPS, SSM_GROUP_CH) * ug
    z = jnp.einsum('blgh,ghk->blgk', jax.nn.gelu(y), w_glu.astype(jnp.float32))
    out = z[..., :SSM_GROUP_CH] * jax.nn.sigmoid(z[..., SSM_GROUP_CH:])
    return out.reshape(Bt, L, SSM_WIDTH).astype(u.dtype)


def pool_mixer(u, pool_w, pool_scale):
    Bt, L, _ = u.shape
    uf = u.astype(jnp.float32)
    cs = jnp.cumsum(uf, axis=1)
    cs = jnp.concatenate([jnp.zeros((Bt, 1, POOL_WIDTH), jnp.float32), cs], axis=1)
    t1 = jnp.arange(1, L + 1, dtype=jnp.float32)
    outs = []
    for gi, w in enumerate(POOL_WINDOWS):
        lo_c, hi_c = gi * POOL_GROUP, (gi + 1) * POOL_GROUP
        c = cs[:, :, lo_c:hi_c]
        hi = c[:, 1:]
        lo = jnp.concatenate([jnp.zeros((Bt, w - 1, POOL_GROUP), jnp.float32), c[:, :L - w + 1]], axis=1)
        cnt = jnp.minimum(t1, float(w))[None, :, None]
        outs.append((hi - lo) / cnt - uf[:, :, lo_c:hi_c])
    pooled = jnp.stack(outs, axis=2)
    mixed = jnp.einsum('blgc,gcd->blgd', pooled, pool_w.astype(jnp.float32))
    mixed = mixed * pool_scale.astype(jnp.float32).reshape(POOL_GROUPS, POOL_GROUP)
    return mixed.reshape(Bt, L, POOL_WIDTH).astype(u.dtype)


def setup_inputs(seed: int = 0) -> dict:
    key = jax.random.key(seed)
    ks = jax.random.split(key, 40)
    f32 = jnp.float32

    def nrm(k, shape, scale):
        return jax.random.normal(k, shape, f32) * scale

    def gain(k, shape):
        return 1.0 + 0.02 * jax.random.normal(k, shape, f32)

    Dp = DEPTH
    G, N, H = SSM_GROUPS, SSM_STATE, SSM_GROUP_CH
    n_idx = jnp.arange(N, dtype=f32)
    return {
        "x": nrm(ks[0], (BATCH, SEQ, D_MODEL), 1.0),
        "meta_tokens": nrm(ks[1], (N_META, D_MODEL), 1.0),
        "ffn1_pre_norm": gain(ks[2], (Dp, D_MODEL)),
        "ffn1_post_norm": gain(ks[3], (Dp, D_MODEL)),
        "ffn1_w_gate": nrm(ks[4], (Dp, D_MODEL, D_FF), D_MODEL ** -0.5),
        "ffn1_w_up": nrm(ks[5], (Dp, D_MODEL, D_FF), D_MODEL ** -0.5),
        "ffn1_w_down": nrm(ks[6], (Dp, D_FF, D_MODEL), D_FF ** -0.5),
        "mix_pre_norm": gain(ks[7], (Dp, D_MODEL)),
        "mix_post_norm": gain(ks[8], (Dp, D_MODEL)),
        "w_in": nrm(ks[9], (Dp, D_MODEL, MIX_WIDTH), D_MODEL ** -0.5),
        "ssm_lambda_re": -0.5 + 0.01 * jax.random.normal(ks[10], (Dp, G, N), f32),
        "ssm_lambda_im": math.pi * n_idx + 0.01 * jax.random.normal(ks[11], (Dp, G, N), f32),
        "ssm_log_dt": jax.random.uniform(ks[12], (Dp, G), f32, math.log(DT_MIN), math.log(DT_MAX)),
        "ssm_b_re": nrm(ks[13], (Dp, G, N, H), (2.0 * H) ** -0.5),
        "ssm_b_im": nrm(ks[14], (Dp, G, N, H), (2.0 * H) ** -0.5),
        "ssm_c_re": nrm(ks[15], (Dp, G, H, N), (2.0 * N) ** -0.5),
        "ssm_c_im": nrm(ks[16], (Dp, G, H, N), (2.0 * N) ** -0.5),
        "ssm_d": nrm(ks[17], (Dp, SSM_WIDTH), 1.0),
        "ssm_w_glu": nrm(ks[18], (Dp, G, H, 2 * H), H ** -0.5),
        "pool_w": nrm(ks[19], (Dp, POOL_GROUPS, POOL_GROUP, POOL_GROUP), POOL_GROUP ** -0.5),
        "pool_scale": gain(ks[20], (Dp, POOL_WIDTH)),
        "ssm_out_norm": gain(ks[21], (Dp, SSM_WIDTH)),
        "pool_out_norm": gain(ks[22], (Dp, POOL_WIDTH)),
        "w_out": nrm(ks[23], (Dp, MIX_WIDTH, D_MODEL), MIX_WIDTH ** -0.5),
        "ffn2_pre_norm": gain(ks[24], (Dp, D_MODEL)),
        "ffn2_post_norm": gain(ks[25], (Dp, D_MODEL)),
        "ffn2_w_gate": nrm(ks[26], (Dp, D_MODEL, D_FF), D_MODEL ** -0.5),
        "ffn2_w_up": nrm(ks[27], (Dp, D_MODEL, D_FF), D_MODEL ** -0.5),
        "ffn2_w_down": nrm(ks[28], (Dp, D_FF, D_MODEL), D_FF ** -0.5),
    }


def reference(x, meta_tokens, ffn1_pre_norm, ffn1_post_norm, ffn1_w_gate, ffn1_w_up, ffn1_w_down,
              mix_pre_norm, mix_post_norm, w_in, ssm_lambda_re, ssm_lambda_im, ssm_log_dt,
              ssm_b_re, ssm_b_im, ssm_c_re, ssm_c_im, ssm_d, ssm_w_glu, pool_w, pool_scale,
              ssm_out_norm, pool_out_norm, w_out, ffn2_pre_norm, ffn2_post_norm,
              ffn2_w_gate, ffn2_w_up, ffn2_w_down):
    Bt = x.shape[0]
    meta = jnp.broadcast_to(meta_tokens.astype(x.dtype)[None], (Bt, N_META, D_MODEL))
    h = jnp.concatenate([meta, x], axis=1)
    for i in range(DEPTH):
        f = swiglu(rms_norm(h, ffn1_pre_norm[i]), ffn1_w_gate[i], ffn1_w_up[i], ffn1_w_down[i])
        h = h + 0.5 * rms_norm(f, ffn1_post_norm[i])
        proj = rms_norm(h, mix_pre_norm[i]) @ w_in[i]
        y_ssm = s5_mixer(proj[..., :SSM_WIDTH], ssm_lambda_re[i], ssm_lambda_im[i], ssm_log_dt[i],
                         ssm_b_re[i], ssm_b_im[i], ssm_c_re[i], ssm_c_im[i], ssm_d[i], ssm_w_glu[i])
        y_pool = pool_mixer(proj[..., SSM_WIDTH:], pool_w[i], pool_scale[i])
        mixed = jnp.concatenate([rms_norm(y_ssm, ssm_out_norm[i]),
                                 rms_norm(y_pool, pool_out_norm[i])], axis=-1) @ w_out[i]
        h = h + rms_norm(mixed, mix_post_norm[i])
        f = swiglu(rms_norm(h, ffn2_pre_norm[i]), ffn2_w_gate[i], ffn2_w_up[i], ffn2_w_down[i])
        h = h + 0.5 * rms_norm(f, ffn2_post_norm[i])
    return h[:, N_META:]
```

```python
import math
import os
import os
import numpy as np
import concourse.bass as bass
import concourse.mybir as mybir
from concourse.bass_utils import run_bass_kernel_spmd

F32 = mybir.dt.float32
BF16 = mybir.dt.bfloat16
I32 = mybir.dt.int32
AF = mybir.ActivationFunctionType
ALU = mybir.AluOpType

D = 1024
DC = 8
FF = 2816
FC = 22
SEQ = 8192
NMETA = 16
LTOT = SEQ + NMETA
T = 432
NT = LTOT // T
TC = 4
J = T // TC
NBLK = 16
EPS = 1e-6
TOKB = [(0, 16), (16, 144), (144, 272), (272, 400), (400, 432)]
TWO_PI = 2.0 * math.pi
MAGIC = 12582912.0
ENGS = ("pe", "act", "dve", "pool", "sp")


class Tok:
    __slots__ = ("sem", "val")

    def __init__(self, sem, val):
        self.sem = sem
        self.val = val


class Buf:
    __slots__ = ("w", "r", "name", "excl")

    def __init__(self, name="", excl=False):
        self.w = None
        self.r = {}
        self.name = name
        self.excl = excl


class Prog:
    def __init__(self, nc):
        self.nc = nc
        self.ops = {e: [] for e in ENGS}
        self.sem = {e: nc.alloc_semaphore("ord_" + e) for e in ENGS}
        self.cnt = {e: 0 for e in ENGS}
        self.waited = {e: {} for e in ENGS}
        self.nsem = 0

    def new_sem(self, name=None):
        self.nsem += 1
        return self.nc.alloc_semaphore(name or ("u%d" % self.nsem))

    def _deps(self, reads, writes, extra):
        deps = list(extra)
        for b in reads:
            if b.w is not None:
                deps.append(b.w)
            if b.excl:
                deps.extend(b.r.values())
        for b in writes:
            if b.w is not None:
                deps.append(b.w)
            deps.extend(b.r.values())
        return deps

    def _waits(self, eng, deps):
        wd = self.waited[eng]
        best = {}
        for d in deps:
            if d is None:
                continue
            if eng == "pe" and d.sem is self.sem["pe"]:
                continue
            k = id(d.sem)
            if k not in best or best[k].val < d.val:
                best[k] = d
        ws = []
        for k, d in best.items():
            if wd.get(k, 0) >= d.val:
                continue
            wd[k] = d.val
            ws.append((d.sem, d.val))
        return ws

    def _mark(self, eng, tok, reads, writes):
        for b in reads:
            b.r[eng + str(id(tok.sem))] = tok
        for b in writes:
            b.w = tok
            b.r = {}

    def op(self, eng, fn, reads=(), writes=(), extra=()):
        ws = self._waits(eng, self._deps(reads, writes, extra))
        self.cnt[eng] += 1
        tok = Tok(self.sem[eng], self.cnt[eng])
        self.ops[eng].append((fn, ws, (self.sem[eng], 1)))
        self._mark(eng, tok, reads, writes)
        return tok

    def group(self, eng, fns, reads=(), writes=(), extra=()):
        ws = self._waits(eng, self._deps(reads, writes, extra))
        self.cnt[eng] += 1
        tok = Tok(self.sem[eng], self.cnt[eng])
        n = len(fns)
        for i, fn in enumerate(fns):
            self.ops[eng].append((fn, ws if i == 0 else [], (self.sem[eng], 1) if i == n - 1 else None))
        self._mark(eng, tok, reads, writes)
        return tok

    def dma(self, eng, out, in_, slot, reads=(), writes=(), extra=(), **kw):
        ws = self._waits(eng, self._deps(reads, writes, extra))
        slot[1] += 16
        tok = Tok(slot[0], slot[1])
        self.ops[eng].append((lambda e: e.dma_start(out=out, in_=in_, **kw), ws, (slot[0], 16)))
        self._mark(eng, tok, reads, writes)
        return tok

    def wait_only(self, eng, deps):
        ws = self._waits(eng, deps)
        if ws:
            self.ops[eng].append((None, ws, None))

    def emit(self):
        nc = self.nc
        with nc.Block() as block:
            def run(e, name):
                for fn, ws, inc in self.ops[name]:
                    for (s, v) in ws:
                        e.wait_ge(s, v)
                    if fn is None:
                        continue
                    ins = fn(e)
                    if inc is not None:
                        ins.then_inc(inc[0], inc[1])

            @block.tensor
            def _(e):
                run(e, "pe")

            @block.scalar
            def _(e):
                run(e, "act")

            @block.vector
            def _(e):
                run(e, "dve")

            @block.gpsimd
            def _(e):
                run(e, "pool")

            @block.sync
            def _(e):
                run(e, "sp")


def build(ntiles=NT, debug=False):
    nc = bass.Bass("TRN2", target_bir_lowering=False)
    P = Prog(nc)

    def din(name, shape):
        return nc.dram_tensor(name, list(shape), F32, kind="ExternalInput").ap()

    x_d = din("x", [SEQ, D])
    meta_d = din("meta_tokens", [NMETA, D])
    gain_names = ["ffn1_pre_norm", "ffn1_post_norm", "mix_pre_norm", "mix_post_norm",
                  "ffn2_pre_norm", "ffn2_post_norm"]
    gains_d = {n: din(n, [D]) for n in gain_names}
    half_names = ["ssm_out_norm", "pool_out_norm", "pool_scale", "ssm_d"]
    halves_d = {n: din(n, [512]) for n in half_names}
    wg_d = [din("ffn1_w_gate", [D, FF]), din("ffn2_w_gate", [D, FF])]
    wu_d = [din("ffn1_w_up", [D, FF]), din("ffn2_w_up", [D, FF])]
    wd_d = [din("ffn1_w_down", [FF, D]), din("ffn2_w_down", [FF, D])]
    win_d = din("w_in", [D, D])
    wout_d = din("w_out", [D, D])
    lre_d = din("ssm_lambda_re", [32, 64])
    lim_d = din("ssm_lambda_im", [32, 64])
    ldt_d = din("ssm_log_dt", [32])
    bre_d = din("ssm_b_re", [32, 64, 16])
    bim_d = din("ssm_b_im", [32, 64, 16])
    cre_d = din("ssm_c_re", [32, 16, 64])
    cim_d = din("ssm_c_im", [32, 16, 64])
    wglu_d = din("ssm_w_glu", [32, 16, 32])
    poolw_d = din("pool_w", [4, 128, 128])
    out_d = nc.dram_tensor("out", [SEQ, D], F32, kind="ExternalOutput").ap()

    def dscr(name, shape):
        return nc.dram_tensor(name, list(shape), BF16, kind="Internal").ap()

    scr_g = [dscr("scr_g%d" % f, [FC, 128, 8, 128]) for f in range(2)]
    scr_u = [dscr("scr_u%d" % f, [FC, 128, 8, 128]) for f in range(2)]
    scr_d = [dscr("scr_d%d" % f, [2, FC, 128, 512]) for f in range(2)]
    scr_in = dscr("scr_in", [8, 128, 8, 128])
    scr_out = dscr("scr_out", [8, 128, 8, 128])

    def sb(name, shape, dt=F32):
        return nc.alloc_sbuf_tensor(name, list(shape), dt)

    Ep = sb("Ep", [128, 2, NBLK, J + 2]);  Ep_b = Buf("Ep")
    Rdec = sb("Rdec", [128, NBLK])
    carry = sb("carry", [128, 2, NBLK]);   carry_b = [Buf("carry%d" % c) for c in range(4)]
    UT = sb("UT", [128, DC, 16 + T], BF16);  UT_b = [Buf("UT%d" % i) for i in range(DC)]
    eps_t = sb("eps_t", [128, 1])
    Kblk = sb("Kblk", [128, 4, TC, 128], BF16)
    KBT = sb("KBT", [128, 4, TC, 2, 128], BF16)
    W4 = sb("W4", [128, NBLK, TC, 2, 32], BF16)
    Wglu = sb("Wglu", [128, 4, 2, 8, 16], BF16)
    Wpool = sb("Wpool", [128, 4, 2, 128], BF16)
    gcol = sb("gcol", [128, 64])
    ident = sb("ident", [128, 128])
    ones_bf = sb("ones_bf", [128, 128], BF16)
    const_b = Buf("const")

    ps = nc.alloc_psum_tensor("ps", [128, 8, 512], F32)
    ps_b = [Buf("ps%d" % i, excl=True) for i in range(8)]
    rot = [0]

    def nbank():
        b = 4 + (rot[0] % 4)
        rot[0] += 1
        return b

    def act(fn, reads=(), writes=(), extra=()):
        return P.op("act", fn, reads, writes, extra)

    def dve(fn, reads=(), writes=(), extra=()):
        return P.op("dve", fn, reads, writes, extra)

    def pool(fn, reads=(), writes=(), extra=()):
        return P.op("pool", fn, reads, writes, extra)

    def pe(fns, reads=(), writes=(), extra=()):
        return P.group("pe", fns, reads, writes, extra)

    def mm(out, lhsT, rhs, start, stop, **kw):
        return lambda e: e.matmul(out, lhsT, rhs, start=start, stop=stop, **kw)

    def tr(out, in_, idn):
        return lambda e: e.transpose(out, in_, idn)

    from contextlib import ExitStack
    es = ExitStack()

    def tmp(name, shape, dt=F32):
        return es.enter_context(nc.sbuf_tensor(name, list(shape), dt))

    dbg_slot = [P.new_sem("dbg"), 0]

    def dump(name, ap, bufs, dt=F32):
        if not debug:
            return
        dd = nc.dram_tensor("dbg_" + name, list(ap.shape), dt, kind="ExternalOutput").ap()
        P.dma("sp", dd, ap, dbg_slot, reads=bufs)
    ZB = tmp("ZB", [128, 2, TC, NBLK, 32])
    ZC = tmp("ZC", [128, 2, NBLK, 32])
    m32 = tmp("m32", [128, 4, 32])
    Ktmp = tmp("Ktmp", [128, 128])
    kconst_b = Buf("kconst")
    su_sem = [P.new_sem("setup"), 0]
    su_b = Buf("setup_loads")

    def ld(out, in_, **kw):
        P.dma("sp", out, in_, su_sem, writes=[su_b], **kw)

    pool(lambda e: e.memset(ident[:, :], 0.0), writes=[const_b])
    pool(lambda e: e.affine_select(out=ident[:, :], in_=ident[:, :], compare_op=ALU.not_equal,
                                   fill=1.0, base=0, pattern=[[-1, 128]], channel_multiplier=1),
         writes=[const_b])
    pool(lambda e: e.memset(ones_bf[:, :], 1.0), writes=[const_b])
    pool(lambda e: e.memset(eps_t[:, :], EPS), writes=[const_b])
    m16 = tmp("m16", [128, 8])
    pool(lambda e: e.memset(m16[:, :], 1.0), writes=[const_b])
    pool(lambda e: e.affine_select(out=m16[:, :], in_=m16[:, :], compare_op=ALU.is_ge, fill=0.0,
                                   base=0, pattern=[[-16, 8]], channel_multiplier=1), writes=[const_b])
    pool(lambda e: e.affine_select(out=m16[:, :], in_=m16[:, :], compare_op=ALU.is_ge, fill=0.0,
                                   base=15, pattern=[[16, 8]], channel_multiplier=-1), writes=[const_b])
    pool(lambda e: e.memset(m32[:, :, :], 1.0), writes=[const_b])
    pool(lambda e: e.affine_select(out=m32[:, :, :], in_=m32[:, :, :], compare_op=ALU.is_ge, fill=0.0,
                                   base=0, pattern=[[-32, 4], [0, 32]], channel_multiplier=1), writes=[const_b])
    pool(lambda e: e.affine_select(out=m32[:, :, :], in_=m32[:, :, :], compare_op=ALU.is_ge, fill=0.0,
                                   base=31, pattern=[[32, 4], [0, 32]], channel_multiplier=-1), writes=[const_b])
    ramp_i = tmp("ramp_i", [128, J + 2], I32)
    ramp = tmp("ramp", [128, J + 2])
    pool(lambda e: e.iota(ramp_i[:, :], pattern=[[1, J + 2]], base=0, channel_multiplier=0), writes=[const_b])
    dve(lambda e: e.tensor_copy(ramp[:, :], ramp_i[:, :]), reads=[const_b], writes=[const_b])
    pool(lambda e: e.memset(carry[:, :, :], 0.0), writes=carry_b)
    pool(lambda e: e.memset(UT[:, :, 0:16], 0.0), writes=UT_b)
    pool(lambda e: e.memset(Kblk[:, :, :, :], 0.0), writes=[const_b])
    pool(lambda e: e.memset(W4[:, :, :, :, :], 0.0), writes=[const_b])

    grow = tmp("grow", [64, 128])
    for i, n in enumerate(gain_names):
        ld(grow[8 * i:8 * i + 8, :], gains_d[n].rearrange("(a p) -> a p", p=128))
    for i, n in enumerate(half_names):
        ld(grow[48 + 4 * i:48 + 4 * i + 4, :], halves_d[n].rearrange("(a p) -> a p", p=128))
    lamrow = tmp("lamrow", [16, 2, 128])
    ld(lamrow[:, 0, :], lre_d.rearrange("(b g) n -> b (g n)", g=2))
    ld(lamrow[:, 1, :], lim_d.rearrange("(b g) n -> b (g n)", g=2))
    dt32 = tmp("dt32", [128, 32])
    ld(dt32[:, :], ldt_d.partition_broadcast(128))
    Braw = tmp("Braw", [128, 2, NBLK, 16])
    ld(Braw[:, 0, :, :], bre_d.rearrange("(b g) n h -> (g n) b h", g=2))
    ld(Braw[:, 1, :, :], bim_d.rearrange("(b g) n h -> (g n) b h", g=2))
    Craw = tmp("Craw", [16, 2, NBLK, 128])
    ld(Craw[:, 0, :, :].rearrange("h b (g n) -> h b g n", g=2), cre_d.rearrange("(b g) h n -> h b g n", g=2))
    ld(Craw[:, 1, :, :].rearrange("h b (g n) -> h b g n", g=2), cim_d.rearrange("(b g) h n -> h b g n", g=2))
    gluraw = tmp("gluraw", [128, 4, 32])
    ld(gluraw[:, :, :], wglu_d.rearrange("(c g) h k -> (g h) c k", c=4))
    praw = tmp("praw", [128, 4, 128])
    ld(praw[:, :, :], poolw_d.rearrange("g c d -> c g d"))
    SU = [su_b]

    cv = {}

    def conv(key, out, in_):
        if key not in cv:
            cv[key] = [P.new_sem("cv_" + key), 0, Buf("cv_" + key)]
        s = cv[key]
        sl = [s[0], s[1]]
        P.dma("pool", out, in_, sl, writes=[s[2]])
        s[1] = sl[1]

    def conv_ffn(f):
        gsrc = wg_d[f].rearrange("(dc p) (fc c) -> fc p dc c", p=128, c=128)
        usrc = wu_d[f].rearrange("(dc p) (fc c) -> fc p dc c", p=128, c=128)
        for fc in range(FC):
            conv("g%d" % f, scr_g[f][fc], gsrc[fc])
            conv("u%d" % f, scr_u[f][fc], usrc[fc])
        dsrc = wd_d[f].rearrange("(fc p) (dh c) -> dh fc p c", p=128, c=512)
        for dh in range(2):
            for fc in range(FC):
                conv("d%d" % f, scr_d[f][dh, fc], dsrc[dh, fc])

    conv_ffn(0)
    isrc = win_d.rearrange("(dc p) (oc c) -> oc p dc c", p=128, c=128)
    osrc = wout_d.rearrange("(dc p) (oc c) -> oc p dc c", p=128, c=128)
    for oc in range(8):
        conv("in", scr_in[oc], isrc[oc])
    for oc in range(8):
        conv("out", scr_out[oc], osrc[oc])
    for key in ("g1", "u1", "d1"):
        cv[key] = [P.new_sem("cv_" + key), 0, Buf("cv_" + key)]

    b0 = 4
    pe([tr(ps[:, b0, 0:64], grow[:, :], ident[0:64, 0:64])], reads=SU + [const_b], writes=[ps_b[b0]])
    dve(lambda e, b0=b0: e.tensor_copy(gcol[:, :], ps[:, b0, 0:64]), reads=[ps_b[b0]], writes=[const_b])
    dump("grow", grow[:, :], SU)
    dump("gcol0", gcol[:, :], [const_b])
    dve(lambda e: e.tensor_scalar_mul(gcol[:, 8:16], gcol[:, 8:16], 0.5), reads=[const_b], writes=[const_b])
    dve(lambda e: e.tensor_scalar_mul(gcol[:, 40:48], gcol[:, 40:48], 0.5), reads=[const_b], writes=[const_b])
    G_F1PRE, G_F1POST, G_MPRE, G_MPOST, G_F2PRE, G_F2POST, G_SSMN, G_POOLN, G_PSCALE, G_SSMD = \
        0, 8, 16, 24, 32, 40, 48, 52, 56, 60

    for g, w in enumerate((2, 4, 8, 16)):
        dve(lambda e, g=g, w=w: e.tensor_scalar_mul(Wpool[:, g, 0, :], praw[:, g, :], 1.0 / w - 1.0),
            reads=SU, writes=[const_b])
        dve(lambda e, g=g, w=w: e.tensor_scalar_mul(Wpool[:, g, 1, :], praw[:, g, :], 1.0 / w),
            reads=SU, writes=[const_b])
    for vg in range(2):
        dve(lambda e, vg=vg: e.tensor_tensor(
            Wglu[:, :, vg, :, :],
            gluraw[:, :, 16 * vg:16 * vg + 16].unsqueeze(2).to_broadcast([128, 4, 8, 16]),
            m16[:, :].unsqueeze(1).unsqueeze(3).to_broadcast([128, 4, 8, 16]), ALU.mult),
            reads=SU + [const_b], writes=[const_b])
    dve(lambda e: e.tensor_scalar_mul(Wglu[:, :, 0, :, :], Wglu[:, :, 0, :, :], 0.5), reads=[const_b], writes=[const_b])

    lam = tmp("lam", [128, 2, NBLK])
    b0 = 5
    pe([tr(ps[:, b0, 0:16], lamrow[:, 0, :], ident[0:16, 0:16]),
        tr(ps[:, b0, 16:32], lamrow[:, 1, :], ident[0:16, 0:16])], reads=SU + [const_b], writes=[ps_b[b0]])
    dve(lambda e, b0=b0: e.tensor_copy(lam[:, :, :], ps[:, b0, 0:32].rearrange("p (a b) -> p a b", a=2)),
        reads=[ps_b[b0]], writes=[const_b])
    CT = tmp("CT", [128, 2, NBLK, 16])
    for pl in range(2):
        b0 = 6 + pl
        pe([tr(ps[:, b0, 16 * b:16 * b + 16], Craw[:, pl, b, :], ident[0:16, 0:16]) for b in range(NBLK)],
           reads=SU + [const_b], writes=[ps_b[b0]])
        dve(lambda e, pl=pl, b0=b0: e.tensor_copy(CT[:, pl, :, :], ps[:, b0, 0:256].rearrange("p (b h) -> p b h", h=16)),
            reads=[ps_b[b0]], writes=[const_b])
    dtl = tmp("dtl", [128, NBLK])
    dtv = tmp("dtv", [128, NBLK])
    dt32v = dt32[:, :].rearrange("p (b g) -> p b g", g=2)
    dve(lambda e: e.tensor_copy(dtl[0:64, :], dt32v[0:64, :, 0]), reads=SU, writes=[const_b])
    dve(lambda e: e.tensor_copy(dtl[64:128, :], dt32v[64:128, :, 1]), reads=SU, writes=[const_b])
    act(lambda e: e.activation(dtv[:, :], dtl[:, :], AF.Exp), reads=[const_b], writes=[const_b])
    e1 = tmp("e1", [128, NBLK])
    ang = tmp("ang", [128, NBLK])
    dve(lambda e: e.tensor_tensor(e1[:, :], lam[:, 0, :], dtv[:, :], ALU.mult), reads=[const_b], writes=[const_b])
    dve(lambda e: e.tensor_tensor(ang[:, :], lam[:, 1, :], dtv[:, :], ALU.mult), reads=[const_b], writes=[const_b])

    NM = 5
    scr1 = tmp("scr1", [128, NBLK * (J + 2)])
    scr2 = tmp("scr2", [128, NBLK * (J + 2)])

    def sincos(dst_sin, dst_cos, src, n):
        a1 = scr1[:, 0:n]
        a2 = scr2[:, 0:n]
        for dst, shift in ((dst_sin, 0.0), (dst_cos, math.pi / 2.0)):
            dve(lambda e, shift=shift: e.tensor_scalar(a1, src, shift, 1.0 / TWO_PI, ALU.add, ALU.mult),
                reads=[const_b], writes=[const_b])
            dve(lambda e: e.tensor_scalar(a1, a1, MAGIC, MAGIC, ALU.add, ALU.subtract), reads=[const_b], writes=[const_b])
            dve(lambda e: e.scalar_tensor_tensor(a2, a1, -TWO_PI, src, ALU.mult, ALU.add), reads=[const_b], writes=[const_b])
            dve(lambda e, shift=shift: e.tensor_scalar(a2, a2, shift, math.pi, ALU.add, ALU.min),
                reads=[const_b], writes=[const_b])
            dve(lambda e: e.tensor_scalar_max(a2, a2, -math.pi), reads=[const_b], writes=[const_b])
            act(lambda e, dst=dst: e.activation(dst, a2, AF.Sin), reads=[const_b], writes=[const_b])

    angm = tmp("angm", [128, NM, NBLK])
    magm = tmp("magm", [128, NM, NBLK])
    PW = tmp("PW", [128, 2, NM, NBLK])
    sn = tmp("sn", [128, NM, NBLK])
    cs = tmp("cs", [128, NM, NBLK])
    for m in range(NM):
        dve(lambda e, m=m: e.tensor_scalar_mul(angm[:, m, :], ang[:, :], float(m)), reads=[const_b], writes=[const_b])
        act(lambda e, m=m: e.activation(magm[:, m, :], e1[:, :], AF.Exp, scale=float(m)), reads=[const_b], writes=[const_b])
    fl = lambda t: t[:, :, :].rearrange("p a b -> p (a b)")
    sincos(fl(sn), fl(cs), fl(angm), NM * NBLK)
    dve(lambda e: e.tensor_tensor(PW[:, 0, :, :], magm[:, :, :], cs[:, :, :], ALU.mult), reads=[const_b], writes=[const_b])
    dve(lambda e: e.tensor_tensor(PW[:, 1, :, :], magm[:, :, :], sn[:, :, :], ALU.mult), reads=[const_b], writes=[const_b])
    dve(lambda e: e.tensor_copy(Rdec[:, :], magm[:, 4, :]), reads=[const_b], writes=[const_b])
    th = tmp("th", [128, NBLK])
    dve(lambda e: e.tensor_scalar(scr1[:, 0:NBLK], angm[:, 4, :], 1.0 / TWO_PI, MAGIC, ALU.mult, ALU.add), reads=[const_b], writes=[const_b])
    dve(lambda e: e.tensor_scalar_sub(scr1[:, 0:NBLK], scr1[:, 0:NBLK], MAGIC), reads=[const_b], writes=[const_b])
    dve(lambda e: e.scalar_tensor_tensor(th[:, :], scr1[:, 0:NBLK], -TWO_PI, angm[:, 4, :], ALU.mult, ALU.add), reads=[const_b], writes=[const_b])
    epang = tmp("epang", [128, NBLK, J + 2])
    dve(lambda e: e.tensor_tensor(epang[:, :, :], th[:, :].unsqueeze(2).to_broadcast([128, NBLK, J + 2]),
                                  ramp[:, :].unsqueeze(1).to_broadcast([128, NBLK, J + 2]), ALU.mult),
        reads=[const_b], writes=[const_b])
    NE = NBLK * (J + 2)
    sincos(Ep[:, 1, :, :].rearrange("p b j -> p (b j)"), Ep[:, 0, :, :].rearrange("p b j -> p (b j)"),
           epang[:, :, :].rearrange("p b j -> p (b j)"), NE)

    t16 = tmp("t16", [128, 8, NBLK])
    cb = [const_b]
    ar, ai = PW[:, 0, 1, :], PW[:, 1, 1, :]
    lr_, li_ = lam[:, 0, :], lam[:, 1, :]
    dve(lambda e: e.tensor_scalar_add(t16[:, 0, :], ar, -1.0), reads=cb, writes=cb)
    dve(lambda e: e.tensor_tensor(t16[:, 1, :], lr_, lr_, ALU.mult), reads=cb, writes=cb)
    dve(lambda e: e.tensor_tensor(t16[:, 2, :], li_, li_, ALU.mult), reads=cb, writes=cb)
    dve(lambda e: e.tensor_tensor(t16[:, 1, :], t16[:, 1, :], t16[:, 2, :], ALU.add), reads=cb, writes=cb)
    dve(lambda e: e.reciprocal(t16[:, 1, :], t16[:, 1, :]), reads=cb, writes=cb)
    dve(lambda e: e.tensor_tensor(t16[:, 2, :], t16[:, 0, :], lr_, ALU.mult), reads=cb, writes=cb)
    dve(lambda e: e.tensor_tensor(t16[:, 3, :], ai, li_, ALU.mult), reads=cb, writes=cb)
    dve(lambda e: e.tensor_tensor(t16[:, 2, :], t16[:, 2, :], t16[:, 3, :], ALU.add), reads=cb, writes=cb)
    dve(lambda e: e.tensor_tensor(t16[:, 4, :], t16[:, 2, :], t16[:, 1, :], ALU.mult), reads=cb, writes=cb)
    dve(lambda e: e.tensor_tensor(t16[:, 2, :], ai, lr_, ALU.mult), reads=cb, writes=cb)
    dve(lambda e: e.tensor_tensor(t16[:, 3, :], t16[:, 0, :], li_, ALU.mult), reads=cb, writes=cb)
    dve(lambda e: e.tensor_tensor(t16[:, 2, :], t16[:, 2, :], t16[:, 3, :], ALU.subtract), reads=cb, writes=cb)
    dve(lambda e: e.tensor_tensor(t16[:, 5, :], t16[:, 2, :], t16[:, 1, :], ALU.mult), reads=cb, writes=cb)
    Bbar = tmp("Bbar", [128, 2, NBLK, 16])
    tB = tmp("tB", [128, 4, TC, NBLK, 16])
    bc16 = lambda ap: ap.unsqueeze(2).to_broadcast([128, NBLK, 16])
    qre, qim = t16[:, 4, :], t16[:, 5, :]
    dve(lambda e: e.tensor_tensor(tB[:, 0, 0, :, :], bc16(qre), Braw[:, 0, :, :], ALU.mult), reads=cb + SU, writes=cb)
    dve(lambda e: e.tensor_tensor(tB[:, 1, 0, :, :], bc16(qim), Braw[:, 1, :, :], ALU.mult), reads=cb + SU, writes=cb)
    dve(lambda e: e.tensor_tensor(Bbar[:, 0, :, :], tB[:, 0, 0, :, :], tB[:, 1, 0, :, :], ALU.subtract), reads=cb, writes=cb)
    dve(lambda e: e.tensor_tensor(tB[:, 0, 0, :, :], bc16(qre), Braw[:, 1, :, :], ALU.mult), reads=cb + SU, writes=cb)
    dve(lambda e: e.tensor_tensor(tB[:, 1, 0, :, :], bc16(qim), Braw[:, 0, :, :], ALU.mult), reads=cb + SU, writes=cb)
    dve(lambda e: e.tensor_tensor(Bbar[:, 1, :, :], tB[:, 0, 0, :, :], tB[:, 1, 0, :, :], ALU.add), reads=cb, writes=cb)

    def cprod(dst, pw_lo, X, conj_sign=1.0):
        shp = [128, TC, NBLK, 16]
        pwb = lambda pl: PW[:, pl, pw_lo:pw_lo + TC, :].unsqueeze(3).to_broadcast(shp)
        xb = lambda pl: X[:, pl, :, :].unsqueeze(1).to_broadcast(shp)
        dve(lambda e: e.tensor_tensor(tB[:, 0, :, :, :], pwb(0), xb(0), ALU.mult), reads=cb, writes=cb)
        dve(lambda e: e.tensor_tensor(tB[:, 1, :, :, :], pwb(1), xb(1), ALU.mult), reads=cb, writes=cb)
        dve(lambda e: e.tensor_tensor(tB[:, 2, :, :, :], pwb(0), xb(1), ALU.mult), reads=cb, writes=cb)
        dve(lambda e: e.tensor_tensor(tB[:, 3, :, :, :], pwb(1), xb(0), ALU.mult), reads=cb, writes=cb)
        dve(lambda e: e.tensor_tensor(dst[:, 0, :, :, :], tB[:, 0, :, :, :], tB[:, 1, :, :, :], ALU.subtract), reads=cb, writes=cb)
        dve(lambda e: e.tensor_tensor(dst[:, 1, :, :, :], tB[:, 2, :, :, :], tB[:, 3, :, :, :], ALU.add), reads=cb, writes=cb)

    Dm = tmp("Dm", [128, 2, TC, NBLK, 16])
    Qm = tmp("Qm", [128, 2, TC, NBLK, 16])
    cprod(Dm, 0, Bbar)
    cprod(Qm, 1, CT)
    dve(lambda e: e.memset(ZB[:, :, :, :, :], 0.0), writes=cb)
    dve(lambda e: e.memset(ZC[:, :, :, :], 0.0), writes=cb)
    for g2 in range(2):
        p0, p1, c0 = 64 * g2, 64 * g2 + 64, 16 * g2
        dve(lambda e, p0=p0, p1=p1, c0=c0: e.tensor_copy(ZB[p0:p1, :, :, :, c0:c0 + 16], Dm[p0:p1, :, :, :, :]), reads=cb, writes=cb)
        dve(lambda e, p0=p0, p1=p1, c0=c0: e.tensor_copy(ZC[p0:p1, 0, :, c0:c0 + 16], CT[p0:p1, 0, :, :]), reads=cb, writes=cb)
        dve(lambda e, p0=p0, p1=p1, c0=c0: e.tensor_scalar_mul(ZC[p0:p1, 1, :, c0:c0 + 16], CT[p0:p1, 1, :, :], -1.0), reads=cb, writes=cb)
        dve(lambda e, p0=p0, p1=p1, c0=c0: e.tensor_copy(
            W4[p0:p1, :, :, 0, c0:c0 + 16], Qm[p0:p1, 0, :, :, :].rearrange("p i b h -> p b i h")), reads=cb, writes=cb)
        dve(lambda e, p0=p0, p1=p1, c0=c0: e.tensor_scalar_mul(
            W4[p0:p1, :, :, 1, c0:c0 + 16], Qm[p0:p1, 1, :, :, :].rearrange("p i b h -> p b i h"), -1.0), reads=cb, writes=cb)
    chain_done = [Tok(P.sem[e_], P.cnt[e_]) for e_ in ("pe", "act", "dve", "pool") if P.cnt[e_] > 0]
    kc = [kconst_b]
    rotK = [0]

    def klag_gen():
        for c in range(4):
            for k in range(TC):
                b0 = rotK[0] % 4
                rotK[0] += 1
                fns = []
                for q in range(4):
                    blk = 4 * c + q
                    o = ps[:, b0, 32 * q:32 * q + 32]
                    fns.append(mm(o, ZB[:, 0, k, 4 * c:4 * c + 4, :].rearrange("p b h -> p (b h)"), ZC[:, 0, blk, :], True, False))
                    fns.append(mm(o, ZB[:, 1, k, 4 * c:4 * c + 4, :].rearrange("p b h -> p (b h)"), ZC[:, 1, blk, :], False, True))
                pe(fns, reads=cb, writes=[ps_b[b0]])
                yield
                m32v = m32[:, :, :].rearrange("p q h -> p (q h)")
                if k == 0:
                    dve(lambda e, c=c, b0=b0: e.tensor_tensor(Ktmp[:, :], ps[:, b0, 0:128], m32v, ALU.mult),
                        reads=[ps_b[b0]] + cb, writes=kc)
                    dve(lambda e, c=c: e.scalar_tensor_tensor(Kblk[:, c, 0, :], ident[:, :], gcol[:, G_SSMD + c:G_SSMD + c + 1],
                                                              Ktmp[:, :], ALU.mult, ALU.add), reads=cb, writes=kc)
                else:
                    dve(lambda e, c=c, k=k, b0=b0: e.tensor_tensor(Kblk[:, c, k, :], ps[:, b0, 0:128], m32v, ALU.mult),
                        reads=[ps_b[b0]] + cb, writes=kc)
        for c in range(4):
            for i in range(TC):
                b0 = rotK[0] % 4
                rotK[0] += 1
                pe([tr(ps[:, b0, 128 * pl:128 * pl + 128],
                       ZB[:, pl, TC - 1 - i, 4 * c:4 * c + 4, :].rearrange("p b h -> p (b h)"), ident[:, :]) for pl in range(2)],
                   reads=cb, writes=[ps_b[b0]])
                yield
                dve(lambda e, c=c, i=i, b0=b0: e.tensor_copy(KBT[:, c, i, :, :], ps[:, b0, 0:256].rearrange("p (a n) -> p a n", a=2)),
                    reads=[ps_b[b0]], writes=kc)

    dump("lam", lam[:, :, :], cb)
    dump("CT", CT[:, :, :, :], cb)
    dump("PW", PW[:, :, :, :], cb)
    dump("Ep", Ep[:, :, :, :], cb)
    dump("Rdec", Rdec[:, :], cb)
    dump("Bbar", Bbar[:, :, :, :], cb)
    dump("Kblk", Kblk[:, :, :, :], cb, BF16)
    dump("KBT", KBT[:, :, :, :, :], cb, BF16)
    dump("W4", W4[:, :, :, :, :], cb, BF16)
    dump("Wglu", Wglu[:, :, :, :, :], cb, BF16)
    es.close()
    setup_done = [Tok(P.sem[e_], P.cnt[e_]) for e_ in ("pe", "act", "dve", "pool") if P.cnt[e_] > 0]

    def bl(name, n):
        return [Buf("%s%d" % (name, i)) for i in range(n)]

    _h1 = sb("hT1", [128, DC, T]); _h2 = sb("hT2", [128, DC, T]); _h0 = sb("hT0", [128, DC, T])
    hTs = [_h0, _h1, _h2]
    hTs_b = [bl("hT%d_" % i, DC) for i in range(3)]
    xnF = sb("xnF", [128, DC, T], BF16);    xnF_b = bl("xnF", DC)
    HT = sb("HT", [128, FC, T], BF16);      HT_b = bl("HT", FC)
    fstg = sb("fstg", [128, DC, T], BF16);  fstg_b = bl("fstg", DC)
    sqF = sb("sqF", [128, DC, T], BF16);    sqF_b = bl("sqF", DC)
    rstdF = sb("rstdF", [128, 2, T]);       rstdF_b = bl("rstdF", 2)
    sgtF = sb("sgtF", [128, 2, T]);         sgtF_b = bl("sgtF", 2)
    xnM = sb("xnM", [128, DC, T], BF16);    xnM_b = bl("xnM", DC)
    stg = sb("stg", [128, DC, T], BF16);    stg_b = bl("stg", DC)
    sqM = sb("sqM", [128, DC, T], BF16);    sqM_b = bl("sqM", DC)
    rstdM = sb("rstdM", [128, 2, T]);       rstdM_b = bl("rstdM", 2)
    sgtM = sb("sgtM", [128, 1, T]);         sgtM_b = bl("sgtM", 1)
    xn2 = sb("xn2", [128, DC, T], BF16);    xn2_b = bl("xn2", DC)
    Wst = sb("Wst", [128, 2, 4, J + 2]);   Wst_b = Buf("Wst")
    Cin = sb("Cin", [128, 2, 4, J]);       Cin_b = Buf("Cin")
    Ssb, Ssb_b = Cin, Cin_b
    tw = sb("tw", [128, 4, 4, J + 1]);     tw_b = Buf("tw")
    Xb = sb("Xb", [128, 2, 4, J], BF16);   Xb_b = Buf("Xb")
    NSLOT = 8
    wsl = sb("wsl", [128, NSLOT, 1024], BF16)
    wsl_b = bl("wsl", NSLOT)
    wsl_sem = [[P.new_sem("wsl%d" % i), 0] for i in range(NSLOT)]
    xin = sb("xin", [128, 2, 512]);   xin_b = bl("xin", 2)
    xin_sem = [[P.new_sem("xin%d" % i), 0] for i in range(2)]
    xo = sb("xo", [128, 2, D]);     xo_b = bl("xo", 2)
    xo_sem = [[P.new_sem("xo%d" % i), 0] for i in range(2)]
    for _b in (hTs_b[0] + xnF_b + HT_b + fstg_b + sqF_b + rstdF_b + sgtF_b + xnM_b + stg_b + sqM_b + rstdM_b
               + sgtM_b + xn2_b + [Wst_b, Cin_b, tw_b, Xb_b] + wsl_b + xin_b + xo_b):
        for _i, _t in enumerate(chain_done):
            _b.r["setup%d" % _i] = _t

    slot_ctr = [0]
    rotF = [0]
    rotM = [0]

    def bankF():
        b_ = 4 + (rotF[0] % 4)
        rotF[0] += 1
        return b_

    def bankM():
        b_ = rotM[0] % 4
        rotM[0] += 1
        return b_

    def stream(src_ap, nelem, cvkey):
        s_ = slot_ctr[0] % NSLOT
        slot_ctr[0] += 1
        P.dma("sp", wsl[:, s_, 0:nelem], src_ap, wsl_sem[s_], reads=[cv[cvkey][2]], writes=[wsl_b[s_]])
        return s_

    class Set:
        pass

    SF = Set(); SF.xn, SF.xn_b, SF.sq, SF.sq_b, SF.rstd, SF.rstd_b, SF.bank = xnF, xnF_b, sqF, sqF_b, rstdF, rstdF_b, bankF
    S1 = Set(); S1.xn, S1.xn_b, S1.sq, S1.sq_b, S1.rstd, S1.rstd_b, S1.bank = xnF, xnF_b, sqM, sqM_b, rstdM, rstdM_b, bankM
    S2 = Set(); S2.xn, S2.xn_b, S2.sq, S2.sq_b, S2.rstd, S2.rstd_b, S2.bank = xn2, xn2_b, sqM, sqM_b, rstdM, rstdM_b, bankM
    SM = Set(); SM.xn, SM.xn_b, SM.sq, SM.sq_b, SM.rstd, SM.rstd_b, SM.bank = xnM, xnM_b, sqM, sqM_b, rstdM, rstdM_b, bankM

    def norm_stats(S, chunks, dn, ri):
        b0 = S.bank()
        n = len(chunks)
        pe([mm(ps[:, b0, 0:T], ones_bf[:, :], S.sq[:, c, :], i == 0, i == n - 1) for i, c in enumerate(chunks)],
           reads=[S.sq_b[c] for c in chunks] + [const_b], writes=[ps_b[b0]])
        yield
        act(lambda e: e.activation(S.rstd[:, ri, :], ps[:, b0, 0:T], AF.Ln, bias=eps_t[:, 0:1], scale=1.0 / dn),
            reads=[ps_b[b0], const_b], writes=[S.rstd_b[ri]])
        act(lambda e: e.activation(S.rstd[:, ri, :], S.rstd[:, ri, :], AF.Exp, scale=-0.5), reads=[], writes=[S.rstd_b[ri]])

    KBS = int(os.environ.get("KBS", "2"))

    def pre_norm(S, hT, hT_b, goff):
        for c in range(DC):
            act(lambda e, c=c: e.activation(S.sq[:, c, :], hT[:, c, :], AF.Square), reads=[hT_b[c]], writes=[S.sq_b[c]])
            if c % 2 == 1:
                yield
        yield ("bsteps", 1)
        yield from norm_stats(S, list(range(DC)), float(D), 0)
        yield ("bsteps", KBS)
        for c in range(DC):
            dve(lambda e, c=c: e.scalar_tensor_tensor(S.xn[:, c, :], hT[:, c, :], gcol[:, goff + c:goff + c + 1],
                                                      S.rstd[:, 0, :], ALU.mult, ALU.mult),
                reads=[hT_b[c], S.rstd_b[0], const_b], writes=[S.xn_b[c]])
            if c % 2 == 1:
                yield

    def post_norm_add(S, hT, hT_b, stgX, stgX_b):
        yield from norm_stats(S, list(range(DC)), float(D), 1)
        yield
        for c in range(DC):
            dve(lambda e, c=c: e.tensor_tensor(stgX[:, c, :], stgX[:, c, :], S.rstd[:, 1, :], ALU.mult),
                reads=[S.rstd_b[1]], writes=[stgX_b[c]])
            dve(lambda e, c=c: e.tensor_tensor(hT[:, c, :], hT[:, c, :], stgX[:, c, :], ALU.add),
                reads=[stgX_b[c]], writes=[hT_b[c]])
            yield

    def evac_post(S, b0, c, goff, stgX, stgX_b):
        act(lambda e: e.activation(S.sq[:, c, :], ps[:, b0, 0:T], AF.Square), reads=[ps_b[b0]], writes=[S.sq_b[c]])
        act(lambda e: e.activation(stgX[:, c, :], ps[:, b0, 0:T], AF.Copy, scale=gcol[:, goff + c:goff + c + 1]),
            reads=[ps_b[b0], const_b], writes=[stgX_b[c]])

    def ffn(f, gpre, gpost, hT, hT_b, dbgt=-1, Sin=None, mark_gu=None, mark_early=None, post=True):
        S = SF
        KF = 9
        if dbgt == 0:
            dump("rstd0", S.rstd[:, 0, :], [S.rstd_b[0]])
            dump("xn1", S.xn[:, :, :], S.xn_b, BF16)
        for fc in range(FC):
            sg = stream(scr_g[f][fc].rearrange("p a c -> p (a c)"), 1024, "g%d" % f)
            su = stream(scr_u[f][fc].rearrange("p a c -> p (a c)"), 1024, "u%d" % f)
            bg, bu = bankF(), bankF()
            wgv = wsl[:, sg, :].rearrange("p (dc c) -> p dc c", c=128)
            wuv = wsl[:, su, :].rearrange("p (dc c) -> p dc c", c=128)
            pe([mm(ps[:, bg, 0:T], wgv[:, dc, :], Sin.xn[:, dc, :], dc == 0, dc == DC - 1) for dc in range(DC)],
               reads=[wsl_b[sg]] + Sin.xn_b, writes=[ps_b[bg]])
            pe([mm(ps[:, bu, 0:T], wuv[:, dc, :], Sin.xn[:, dc, :], dc == 0, dc == DC - 1) for dc in range(DC)],
               reads=[wsl_b[su]] + Sin.xn_b, writes=[ps_b[bu]])
            si = fc % 2
            act(lambda e, bg=bg, si=si: e.activation(sgtF[:, si, :], ps[:, bg, 0:T], AF.Silu),
                reads=[ps_b[bg]], writes=[sgtF_b[si]])
            dve(lambda e, bu=bu, si=si, fc=fc: e.tensor_tensor(HT[:, fc, :], sgtF[:, si, :], ps[:, bu, 0:T], ALU.mult),
                reads=[sgtF_b[si], ps_b[bu]], writes=[HT_b[fc]])
            if fc == 4 and mark_early is not None:
                yield ("set", mark_early)
            else:
                yield
        if mark_gu is not None:
            yield ("set", mark_gu)
        for dh in range(2):
            banks = [4, 5, 6, 7]
            for fc in range(FC):
                sd = stream(scr_d[f][dh, fc], 512, "d%d" % f)
                for i in range(4):
                    pe([mm(ps[:, banks[i], 0:T], wsl[:, sd, 128 * i:128 * i + 128], HT[:, fc, :], fc == 0, fc == FC - 1)],
                       reads=[wsl_b[sd], HT_b[fc]], writes=[ps_b[banks[i]]])
                if fc % 4 == 3:
                    yield
            for i in range(4):
                evac_post(S, banks[i], 4 * dh + i, gpost, fstg, fstg_b)
            yield
        if post:
            yield from post_norm_add(S, hT, hT_b, fstg, fstg_b)

    def mixer(ti, hT, hT_b):
        S = SM
        xn, xn_b, sq, sq_b, rstd, rstd_b = S.xn, S.xn_b, S.sq, S.sq_b, S.rstd, S.rstd_b
        gl, gl_b = xnM, xnM_b
        yield from pre_norm(S, hT, hT_b, G_MPRE)
        pend = None
        for oc in range(8):
            s_ = stream(scr_in[oc].rearrange("p a c -> p (a c)"), 1024, "in")
            wv = wsl[:, s_, :].rearrange("p (dc c) -> p dc c", c=128)
            b0 = bankM()
            pe([mm(ps[:, b0, 0:T], wv[:, dc, :], xn[:, dc, :], dc == 0, dc == DC - 1) for dc in range(DC)],
               reads=[wsl_b[s_]] + xn_b, writes=[ps_b[b0]])
            if pend is not None:
                pend()
            pend = (lambda b0=b0, oc=oc: act(lambda e: e.activation(UT[:, oc, 16:16 + T], ps[:, b0, 0:T], AF.Copy),
                                           reads=[ps_b[b0]], writes=[UT_b[oc]]))
            yield
        pend()
        yield

        def s_mm(c):
            Uv = UT[:, c, 16:16 + T].rearrange("p (j i) -> p j i", i=TC)
            fns = []
            for q in range(4):
                for pl in range(2):
                    for i in range(TC):
                        kw = {"tile_position": (96, 0)} if q == 3 else {}
                        fns.append(mm(ps[:, q, 256 * pl:256 * pl + J], KBT[32 * q:32 * q + 32, c, i, pl, :],
                                      Uv[32 * q:32 * q + 32, :, i], i == 0, i == TC - 1, **kw))
            pe(fns, reads=[UT_b[c], const_b, kconst_b], writes=[ps_b[q] for q in range(4)])
            yield
            act(lambda e: e.activation(Cin[:, :, :, :].rearrange("p a q j -> p q a j"),
                                       ps[:, 0:4, 0:512].rearrange("p q (a j) -> p q a j", a=2)[:, :, :, 0:J], AF.Copy),
                reads=[ps_b[q] for q in range(4)], writes=[Cin_b])

        def pool_g(g, w):
            oc = 4 + g
            b0 = bankM()
            pe([mm(ps[:, b0, 0:T], Wpool[:, g, 0 if k == 0 else 1, :], UT[:, oc, 16 - k:16 - k + T], k == 0, k == w - 1)
                for k in range(w)], reads=[UT_b[oc], const_b], writes=[ps_b[b0]])
            yield
            act(lambda e: e.activation(stg[:, oc, :], ps[:, b0, 0:T], AF.Copy,
                                       scale=gcol[:, G_PSCALE + oc - 4:G_PSCALE + oc - 3]),
                reads=[ps_b[b0], const_b], writes=[stg_b[oc]])
            act(lambda e: e.activation(sq[:, oc, :], stg[:, oc, :], AF.Square), reads=[stg_b[oc]], writes=[sq_b[oc]])
            dve(lambda e: e.tensor_copy(UT[:, oc, 0:16], UT[:, oc, T:T + 16]), reads=[], writes=[UT_b[oc]])

        def ssm_chunk(c):
            Uv = UT[:, c, 16:16 + T].rearrange("p (j i) -> p j i", i=TC)
            yield from s_mm(c)
            yield
            Sre = Cin[:, 0, :, :]
            Sim = Cin[:, 1, :, :]
            blks = slice(4 * c, 4 * c + 4)
            Epr1, Epi1 = Ep[:, 0, blks, 1:J + 1], Ep[:, 1, blks, 1:J + 1]
            sread = [Cin_b, const_b]
            dve(lambda e: e.tensor_tensor(tw[:, 0, :, 0:J], Sre, Epr1, ALU.mult), reads=sread, writes=[tw_b])
            dve(lambda e: e.tensor_tensor(tw[:, 1, :, 0:J], Sim, Epi1, ALU.mult), reads=sread, writes=[tw_b])
            yield
            dve(lambda e: e.tensor_tensor(tw[:, 2, :, 0:J], Sim, Epr1, ALU.mult), reads=sread, writes=[tw_b])
            dve(lambda e: e.tensor_tensor(tw[:, 3, :, 0:J], Sre, Epi1, ALU.mult), reads=sread, writes=[tw_b])
            yield
            dve(lambda e: e.tensor_tensor(Cin[:, 0, :, :], tw[:, 0, :, 0:J], tw[:, 1, :, 0:J], ALU.add), reads=[tw_b], writes=[Cin_b])
            dve(lambda e: e.tensor_tensor(Cin[:, 1, :, :], tw[:, 2, :, 0:J], tw[:, 3, :, 0:J], ALU.subtract), reads=[tw_b], writes=[Cin_b])
            dve(lambda e: e.tensor_copy(Wst[:, :, :, 0], carry[:, :, blks]), reads=[carry_b[c]], writes=[Wst_b])
            yield
            for pl in range(2):
                for q in range(4):
                    blk = 4 * c + q
                    dve(lambda e, pl=pl, q=q, blk=blk: e.tensor_tensor_scan(
                        Wst[:, pl, q, 1:J + 1], Rdec[:, blk:blk + 1].to_broadcast([128, J]), Cin[:, pl, q, :],
                        Wst[:, pl, q, 0:1], ALU.mult, ALU.add), reads=[Cin_b, const_b], writes=[Wst_b])
                    if q % 2 == 1:
                        yield
            Epr0, Epi0 = Ep[:, 0, blks, 0:J + 1], Ep[:, 1, blks, 0:J + 1]
            Wre, Wim = Wst[:, 0, :, 0:J + 1], Wst[:, 1, :, 0:J + 1]
            dve(lambda e: e.tensor_tensor(tw[:, 0, :, :], Wre, Epr0, ALU.mult), reads=[Wst_b, const_b], writes=[tw_b])
            dve(lambda e: e.tensor_tensor(tw[:, 1, :, :], Wim, Epi0, ALU.mult), reads=[Wst_b, const_b], writes=[tw_b])
            yield
            dve(lambda e: e.tensor_tensor(tw[:, 2, :, :], Wim, Epr0, ALU.mult), reads=[Wst_b, const_b], writes=[tw_b])
            dve(lambda e: e.tensor_tensor(tw[:, 3, :, :], Wre, Epi0, ALU.mult), reads=[Wst_b, const_b], writes=[tw_b])
            yield
            dve(lambda e: e.tensor_tensor(Xb[:, 0, :, :], tw[:, 0, :, 0:J], tw[:, 1, :, 0:J], ALU.subtract), reads=[tw_b], writes=[Xb_b])
            dve(lambda e: e.tensor_tensor(Xb[:, 1, :, :], tw[:, 2, :, 0:J], tw[:, 3, :, 0:J], ALU.add), reads=[tw_b], writes=[Xb_b])
            yield
            dve(lambda e: e.tensor_tensor(carry[:, 0, blks], tw[:, 0, :, J], tw[:, 1, :, J], ALU.subtract), reads=[tw_b], writes=[carry_b[c]])
            dve(lambda e: e.tensor_tensor(carry[:, 1, blks], tw[:, 2, :, J], tw[:, 3, :, J], ALU.add), reads=[tw_b], writes=[carry_b[c]])
            yield ("bsteps", KBS)
            by = bankM()
            Yv = ps[:, by, 0:T].rearrange("p (j i) -> p j i", i=TC)
            fns = []
            for k in range(TC):
                fns.append(mm(Yv[:, :, k:TC], Kblk[:, c, k, :], Uv[:, :, 0:TC - k], k == 0, False, skip_group_check=True))
            for q in range(4):
                blk = 4 * c + q
                for i in range(TC):
                    for pl in range(2):
                        last = (q == 3 and i == TC - 1 and pl == 1)
                        kw = {"tile_position": (0, 96)} if q == 3 else {}
                        fns.append(mm(Yv[32 * q:32 * q + 32, :, i], W4[:, blk, i, pl, :], Xb[:, pl, q, :], False, last,
                                      skip_group_check=True, **kw))
            pe(fns, reads=[UT_b[c], Xb_b, const_b, kconst_b], writes=[ps_b[by]])
            yield
            act(lambda e: e.activation(gl[:, c, :], ps[:, by, 0:T], AF.Gelu_apprx_tanh), reads=[ps_b[by]], writes=[gl_b[c]])
            yield
            bv, bg = bankM(), bankM()
            pe([mm(ps[:, bv, 0:T], Wglu[:, c, 0, :, :].rearrange("p g k -> p (g k)"), gl[:, c, :], True, True)],
               reads=[gl_b[c], const_b], writes=[ps_b[bv]])
            pe([mm(ps[:, bg, 0:T], Wglu[:, c, 1, :, :].rearrange("p g k -> p (g k)"), gl[:, c, :], True, True)],
               reads=[gl_b[c], const_b], writes=[ps_b[bg]])
            yield
            si = 0
            act(lambda e: e.activation(sgtM[:, si, :], ps[:, bg, 0:T], AF.Tanh, scale=0.5), reads=[ps_b[bg]], writes=[sgtM_b[si]])
            yield
            dve(lambda e: e.scalar_tensor_tensor(stg[:, c, :], sgtM[:, si, :], 1.0, ps[:, bv, 0:T], ALU.add, ALU.mult),
                reads=[sgtM_b[si], ps_b[bv]], writes=[stg_b[c]])
            act(lambda e: e.activation(sq[:, c, :], stg[:, c, :], AF.Square), reads=[stg_b[c]], writes=[sq_b[c]])
            yield

        for c_ in range(4):
            yield from ssm_chunk(c_)
            yield from pool_g(c_, (2, 4, 8, 16)[c_])
            yield
        if ti == 0:
            dump("UT", UT[:, :, :], UT_b, BF16)
            dump("ymix", stg[:, :, :], stg_b)
        yield from norm_stats(S, [0, 1, 2, 3], 512.0, 0)
        yield from norm_stats(S, [4, 5, 6, 7], 512.0, 1)
        yield
        for c in range(DC):
            ri = 0 if c < 4 else 1
            go = (G_SSMN + c) if c < 4 else (G_POOLN + c - 4)
            dve(lambda e, c=c, ri=ri, go=go: e.scalar_tensor_tensor(xn[:, c, :], stg[:, c, :], gcol[:, go:go + 1],
                                                                    rstd[:, ri, :], ALU.mult, ALU.mult),
                reads=[stg_b[c], rstd_b[ri], const_b], writes=[xn_b[c]])
            if c % 2 == 1:
                yield
        pend = None
        for oc in range(8):
            s_ = stream(scr_out[oc].rearrange("p a c -> p (a c)"), 1024, "out")
            wv = wsl[:, s_, :].rearrange("p (dc c) -> p dc c", c=128)
            b0 = bankM()
            pe([mm(ps[:, b0, 0:T], wv[:, dc, :], xn[:, dc, :], dc == 0, dc == DC - 1) for dc in range(DC)],
               reads=[wsl_b[s_]] + xn_b, writes=[ps_b[b0]])
            if pend is not None:
                pend()
            pend = (lambda b0=b0, oc=oc: evac_post(S, b0, oc, G_MPOST, stg, stg_b))
            yield
        pend()
        yield
        yield from post_norm_add(S, hT, hT_b, stg, stg_b)

    ld_ctr = [0]

    def load_items():
        items = []
        for bi, (a, b) in enumerate(TOKB):
            for hh in range(2):
                items.append((bi, a, b, hh))
        return items

    def load_issue(ti, k):
        bi, a, b, hh = load_items()[k]
        nb = b - a
        s_ = (ld_ctr[0] + k) % 2
        if ti == 0 and bi == 0:
            src = meta_d[0:16, 512 * hh:512 * hh + 512]
        else:
            r0 = ti * T - NMETA + a
            src = x_d[r0:r0 + nb, 512 * hh:512 * hh + 512]
        P.dma("sp", xin[0:nb, s_, :], src, xin_sem[s_], writes=[xin_b[s_]])

    def load_head(ti):
        load_issue(ti, 0)
        load_issue(ti, 1)
        yield

    def load_tile(ti, hT, hT_b, bank, head_done=False, gap=0):
        items = load_items()
        if not head_done:
            load_issue(ti, 0)
            load_issue(ti, 1)
        for k, (bi, a, b, hh) in enumerate(items):
            nb = b - a
            s_ = (ld_ctr[0] + k) % 2
            b0 = bank()
            pe([tr(ps[:, b0, 128 * i:128 * i + nb], xin[0:nb, s_, 128 * i:128 * i + 128], ident[0:nb, 0:nb])
                for i in range(4)], reads=[xin_b[s_], const_b], writes=[ps_b[b0]])
            yield
            dve(lambda e, b0=b0, hh=hh, a=a, b=b, nb=nb: e.tensor_copy(
                hT[:, 4 * hh:4 * hh + 4, a:b], ps[:, b0, :].rearrange("p (i t) -> p i t", t=128)[:, :, 0:nb]),
                reads=[ps_b[b0]], writes=[hT_b[4 * hh + i] for i in range(4)])
            if k + 2 < len(items):
                load_issue(ti, k + 2)
            for _ in range(gap):
                yield
        ld_ctr[0] += len(items)

    def store_tile(ti, hT, hT_b, bank, gap=0):
        for bi, (a, b) in enumerate(TOKB):
            nb = b - a
            if ti == 0 and bi == 0:
                continue
            s_ = bi % 2
            bks = []
            for hh in range(2):
                b0 = bank()
                bks.append(b0)
                pe([tr(ps[0:nb, b0, 128 * i:128 * i + 128], hT[:, 4 * hh + i, a:b], ident[:, :]) for i in range(4)],
                   reads=[hT_b[4 * hh + i] for i in range(4)] + [const_b], writes=[ps_b[b0]])
            yield
            for hh in range(2):
                b0 = bks[hh]
                act(lambda e, b0=b0, hh=hh, nb=nb, s_=s_: e.activation(xo[0:nb, s_, 512 * hh:512 * hh + 512], ps[0:nb, b0, :], AF.Copy),
                    reads=[ps_b[b0]], writes=[xo_b[s_]])
            r0 = ti * T - NMETA + a
            P.dma("pool", out_d[r0:r0 + nb, :], xo[0:nb, s_, :], xo_sem[s_], reads=[xo_b[s_]])
            for _ in range(gap):
                yield

    def run(g):
        for _ in g:
            pass

    def interleave(ga, gb, na, nb_):
        marks = set()
        st = {"a": [ga, 0, True, None], "b": [gb, 0, True, None]}

        def step(k):
            g = st[k]
            if g[3] is not None:
                if g[3] in marks:
                    g[3] = None
                else:
                    return False
            try:
                r = next(g[0])
            except StopIteration:
                g[2] = False
                return True
            g[1] += 1
            if isinstance(r, tuple):
                if r[0] == "set":
                    marks.add(r[1])
                elif r[0] == "wait" and r[1] not in marks:
                    g[3] = r[1]
                elif r[0] == "bsteps" and k == "a":
                    force[0] = r[1]
            return True

        force = [0]
        while st["a"][2] or st["b"][2]:
            pick = "a" if (st["a"][2] and (not st["b"][2] or st["a"][1] * nb_ <= st["b"][1] * na)) else "b"
            if force[0] > 0 and st["b"][2] and st["b"][3] is None:
                pick = "b"
                force[0] -= 1
            if not step(pick):
                other = "b" if pick == "a" else "a"
                assert st[other][2], "interleave deadlock"
                if not step(other):
                    raise RuntimeError("interleave deadlock")
        if os.environ.get("KDBG"):
            print("interleave steps A=%d B=%d" % (st["a"][1], st["b"][1]))

    def chain(*gs):
        for g in gs:
            if isinstance(g, tuple):
                yield g
            else:
                yield from g

    def zipg(g1, g2, n1=1, n2=1):
        gens = [[g1, n1, True], [g2, n2, True]]
        while gens[0][2] or gens[1][2]:
            for g in gens:
                if not g[2]:
                    continue
                for _ in range(g[1]):
                    try:
                        r = next(g[0])
                    except StopIteration:
                        g[2] = False
                        break
                    yield r

    H = lambda t: (hTs[t % 3], hTs_b[t % 3])
    pn1 = lambda t: pre_norm(S1, *H(t), G_F1PRE)
    pn2 = lambda t: pre_norm(S2, *H(t), G_F2PRE)
    f1 = lambda t, **kw: ffn(0, G_F1PRE, G_F1POST, *H(t), Sin=S1, **kw)
    f2 = lambda t, **kw: ffn(1, G_F2PRE, G_F2POST, *H(t), Sin=S2, **kw)
    fpost = lambda t: post_norm_add(SF, *H(t), fstg, fstg_b)
    run(load_tile(0, *H(0), bankF))
    conv_ffn(1)
    run(pn1(0))
    def set_late():
        late_done = [Tok(P.sem[e_], P.cnt[e_]) for e_ in ("pe", "act", "dve", "pool") if P.cnt[e_] > 0]
        for _b in hTs_b[1] + hTs_b[2]:
            for _i, _t in enumerate(late_done):
                _b.r["late%d" % _i] = _t
        yield

    ga = [klag_gen(), set_late()]
    if ntiles > 1:
        ga += [load_tile(1, *H(1), bankM), ("wait", "f1gu"), pn1(1)]
    interleave(chain(*ga), f1(0, mark_gu="f1gu"), 54, 45)
    ga = [mixer(0, *H(0)), pn2(0)]
    if ntiles > 2:
        ga += [load_tile(2, *H(2), bankM), ("wait", "f1gu"), pn1(2)]
    if ntiles > 1:
        interleave(chain(*ga), chain(f1(1, mark_gu="f1gu")), 156, 45)
    else:
        run(chain(*ga))
    pending = None
    for t in range(ntiles):
        gb = []
        if pending is not None:
            gb.append(zipg(chain(pending, ("set", "p1done")), f2(t, mark_early="b_early", post=False), 2, 1))
        else:
            gb += [("set", "p1done"), f2(t, mark_early="b_early", post=False)]
        if t + 2 < ntiles:
            gb.append(zipg(chain(fpost(t), ("set", "f2done")), f1(t + 2, mark_gu="f1gu", mark_early="f1_early", post=False), 2, 1))
            pending = fpost(t + 2)
        else:
            gb += [fpost(t), ("set", "f2done"), ("set", "f1gu"), ("set", "f1_early")]
            pending = None
        ga = [("wait", "b_early")]
        if t + 3 < ntiles:
            ga.append(load_head(t + 3))
        if t + 1 < ntiles:
            ga.append(("wait", "p1done"))
            if os.environ.get("KNOA") != "1":
                ga += [mixer(t + 1, *H(t + 1))]
            ga += [pn2(t + 1)]
        ga += [("wait", "f1_early"), ("wait", "f2done"), store_tile(t, *H(t), bankM, gap=2)]
        if t + 3 < ntiles:
            ga += [load_tile(t + 3, *H(t + 3), bankM, head_done=True, gap=1), ("wait", "f1gu"), pn1(t + 3)]
        interleave(chain(*ga), chain(*gb), 186, 91)
    assert pending is None

    P.wait_only("pool", [t_ for b_ in xo_b for t_ in b_.r.values()])
    if debug:
        P.wait_only("sp", [Tok(dbg_slot[0], dbg_slot[1])])
    P.emit()
    return nc


_NC_CACHE = {}


def kernel(**inputs):
    if "nc" not in _NC_CACHE:
        _NC_CACHE["nc"] = build()
    nc = _NC_CACHE["nc"]
    f32 = lambda a: np.ascontiguousarray(np.asarray(a, dtype=np.float32))
    x = f32(inputs["x"])
    B = x.shape[0]
    shared = {}
    for k, v in inputs.items():
        if k == "x":
            continue
        a = f32(v)
        if k != "meta_tokens":
            a = a[0]
        shared[k] = np.ascontiguousarray(a)
    in_maps = []
    for b in range(B):
        m = dict(shared)
        m["x"] = x[b]
        in_maps.append(m)
    res = run_bass_kernel_spmd(nc, in_maps, core_ids=list(range(B)))
    out = np.stack([np.asarray(r["out"], dtype=np.float32) for r in res.results], axis=0)
    return out
```

```python
import math
import os
import os
import numpy as np
import concourse.bass as bass
import concourse.mybir as mybir
from concourse.bass_utils import run_bass_kernel_spmd

F32 = mybir.dt.float32
BF16 = mybir.dt.bfloat16
I32 = mybir.dt.int32
AF = mybir.ActivationFunctionType
ALU = mybir.AluOpType

D = 1024
DC = 8
FF = 2816
FC = 22
SEQ = 8192
NMETA = 16
LTOT = SEQ + NMETA
T = 432
NT = LTOT // T
TC = 4
J = T // TC
NBLK = 16
EPS = 1e-6
TOKB = [(0, 16), (16, 144), (144, 272), (272, 400), (400, 432)]
TWO_PI = 2.0 * math.pi
MAGIC = 12582912.0
ENGS = ("pe", "act", "dve", "pool", "sp")


class Tok:
    __slots__ = ("sem", "val")

    def __init__(self, sem, val):
        self.sem = sem
        self.val = val


class Buf:
    __slots__ = ("w", "r", "name", "excl")

    def __init__(self, name="", excl=False):
        self.w = None
        self.r = {}
        self.name = name
        self.excl = excl


class Prog:
    def __init__(self, nc):
        self.nc = nc
        self.ops = {e: [] for e in ENGS}
        self.sem = {e: nc.alloc_semaphore("ord_" + e) for e in ENGS}
        self.cnt = {e: 0 for e in ENGS}
        self.waited = {e: {} for e in ENGS}
        self.nsem = 0

    def new_sem(self, name=None):
        self.nsem += 1
        return self.nc.alloc_semaphore(name or ("u%d" % self.nsem))

    def _deps(self, reads, writes, extra):
        deps = list(extra)
        for b in reads:
            if b.w is not None:
                deps.append(b.w)
            if b.excl:
                deps.extend(b.r.values())
        for b in writes:
            if b.w is not None:
                deps.append(b.w)
            deps.extend(b.r.values())
        return deps

    def _waits(self, eng, deps):
        wd = self.waited[eng]
        best = {}
        for d in deps:
            if d is None:
                continue
            if eng == "pe" and d.sem is self.sem["pe"]:
                continue
            k = id(d.sem)
            if k not in best or best[k].val < d.val:
                best[k] = d
        ws = []
        for k, d in best.items():
            if wd.get(k, 0) >= d.val:
                continue
            wd[k] = d.val
            ws.append((d.sem, d.val))
        return ws

    def _mark(self, eng, tok, reads, writes):
        for b in reads:
            b.r[eng + str(id(tok.sem))] = tok
        for b in writes:
            b.w = tok
            b.r = {}

    def op(self, eng, fn, reads=(), writes=(), extra=()):
        ws = self._waits(eng, self._deps(reads, writes, extra))
        self.cnt[eng] += 1
        tok = Tok(self.sem[eng], self.cnt[eng])
        self.ops[eng].append((fn, ws, (self.sem[eng], 1)))
        self._mark(eng, tok, reads, writes)
        return tok

    def group(self, eng, fns, reads=(), writes=(), extra=()):
        ws = self._waits(eng, self._deps(reads, writes, extra))
        self.cnt[eng] += 1
        tok = Tok(self.sem[eng], self.cnt[eng])
        n = len(fns)
        for i, fn in enumerate(fns):
            self.ops[eng].append((fn, ws if i == 0 else [], (self.sem[eng], 1) if i == n - 1 else None))
        self._mark(eng, tok, reads, writes)
        return tok

    def dma(self, eng, out, in_, slot, reads=(), writes=(), extra=(), **kw):
        ws = self._waits(eng, self._deps(reads, writes, extra))
        slot[1] += 16
        tok = Tok(slot[0], slot[1])
        self.ops[eng].append((lambda e: e.dma_start(out=out, in_=in_, **kw), ws, (slot[0], 16)))
        self._mark(eng, tok, reads, writes)
        return tok

    def wait_only(self, eng, deps):
        ws = self._waits(eng, deps)
        if ws:
            self.ops[eng].append((None, ws, None))

    def emit(self):
        nc = self.nc
        with nc.Block() as block:
            def run(e, name):
                for fn, ws, inc in self.ops[name]:
                    for (s, v) in ws:
                        e.wait_ge(s, v)
                    if fn is None:
                        continue
                    ins = fn(e)
                    if inc is not None:
                        ins.then_inc(inc[0], inc[1])

            @block.tensor
            def _(e):
                run(e, "pe")

            @block.scalar
            def _(e):
                run(e, "act")

            @block.vector
            def _(e):
                run(e, "dve")

            @block.gpsimd
            def _(e):
                run(e, "pool")

            @block.sync
            def _(e):
                run(e, "sp")


def build(ntiles=NT, debug=False):
    nc = bass.Bass("TRN2", target_bir_lowering=False)
    P = Prog(nc)

    def din(name, shape):
        return nc.dram_tensor(name, list(shape), F32, kind="ExternalInput").ap()

    x_d = din("x", [SEQ, D])
    meta_d = din("meta_tokens", [NMETA, D])
    gain_names = ["ffn1_pre_norm", "ffn1_post_norm", "mix_pre_norm", "mix_post_norm",
                  "ffn2_pre_norm", "ffn2_post_norm"]
    gains_d = {n: din(n, [D]) for n in gain_names}
    half_names = ["ssm_out_norm", "pool_out_norm", "pool_scale", "ssm_d"]
    halves_d = {n: din(n, [512]) for n in half_names}
    wg_d = [din("ffn1_w_gate", [D, FF]), din("ffn2_w_gate", [D, FF])]
    wu_d = [din("ffn1_w_up", [D, FF]), din("ffn2_w_up", [D, FF])]
    wd_d = [din("ffn1_w_down", [FF, D]), din("ffn2_w_down", [FF, D])]
    win_d = din("w_in", [D, D])
    wout_d = din("w_out", [D, D])
    lre_d = din("ssm_lambda_re", [32, 64])
    lim_d = din("ssm_lambda_im", [32, 64])
    ldt_d = din("ssm_log_dt", [32])
    bre_d = din("ssm_b_re", [32, 64, 16])
    bim_d = din("ssm_b_im", [32, 64, 16])
    cre_d = din("ssm_c_re", [32, 16, 64])
    cim_d = din("ssm_c_im", [32, 16, 64])
    wglu_d = din("ssm_w_glu", [32, 16, 32])
    poolw_d = din("pool_w", [4, 128, 128])
    out_d = nc.dram_tensor("out", [SEQ, D], F32, kind="ExternalOutput").ap()

    def dscr(name, shape):
        return nc.dram_tensor(name, list(shape), BF16, kind="Internal").ap()

    scr_g = [dscr("scr_g%d" % f, [FC, 128, 8, 128]) for f in range(2)]
    scr_u = [dscr("scr_u%d" % f, [FC, 128, 8, 128]) for f in range(2)]
    scr_d = [dscr("scr_d%d" % f, [2, FC, 128, 512]) for f in range(2)]
    scr_in = dscr("scr_in", [8, 128, 8, 128])
    scr_out = dscr("scr_out", [8, 128, 8, 128])

    def sb(name, shape, dt=F32):
        return nc.alloc_sbuf_tensor(name, list(shape), dt)

    Ep = sb("Ep", [128, 2, NBLK, J + 2]);  Ep_b = Buf("Ep")
    Rdec = sb("Rdec", [128, NBLK])
    carry = sb("carry", [128, 2, NBLK]);   carry_b = [Buf("carry%d" % c) for c in range(4)]
    UT = sb("UT", [128, DC, 16 + T], BF16);  UT_b = [Buf("UT%d" % i) for i in range(DC)]
    eps_t = sb("eps_t", [128, 1])
    Kblk = sb("Kblk", [128, 4, TC, 128], BF16)
    KBT = sb("KBT", [128, 4, TC, 2, 128], BF16)
    W4 = sb("W4", [128, NBLK, TC, 2, 32], BF16)
    Wglu = sb("Wglu", [128, 4, 2, 8, 16], BF16)
    Wpool = sb("Wpool", [128, 4, 2, 128], BF16)
    gcol = sb("gcol", [128, 64])
    ident = sb("ident", [128, 128])
    ones_bf = sb("ones_bf", [128, 128], BF16)
    const_b = Buf("const")

    ps = nc.alloc_psum_tensor("ps", [128, 8, 512], F32)
    ps_b = [Buf("ps%d" % i, excl=True) for i in range(8)]
    rot = [0]

    def nbank():
        b = 4 + (rot[0] % 4)
        rot[0] += 1
        return b

    def act(fn, reads=(), writes=(), extra=()):
        return P.op("act", fn, reads, writes, extra)

    def dve(fn, reads=(), writes=(), extra=()):
        return P.op("dve", fn, reads, writes, extra)

    def pool(fn, reads=(), writes=(), extra=()):
        return P.op("pool", fn, reads, writes, extra)

    def pe(fns, reads=(), writes=(), extra=()):
        return P.group("pe", fns, reads, writes, extra)

    def mm(out, lhsT, rhs, start, stop, **kw):
        return lambda e: e.matmul(out, lhsT, rhs, start=start, stop=stop, **kw)

    def tr(out, in_, idn):
        return lambda e: e.transpose(out, in_, idn)

    from contextlib import ExitStack
    es = ExitStack()

    def tmp(name, shape, dt=F32):
        return es.enter_context(nc.sbuf_tensor(name, list(shape), dt))

    dbg_slot = [P.new_sem("dbg"), 0]

    def dump(name, ap, bufs, dt=F32):
        if not debug:
            return
        dd = nc.dram_tensor("dbg_" + name, list(ap.shape), dt, kind="ExternalOutput").ap()
        P.dma("sp", dd, ap, dbg_slot, reads=bufs)
    ZB = tmp("ZB", [128, 2, TC, NBLK, 32])
    ZC = tmp("ZC", [128, 2, NBLK, 32])
    m32 = tmp("m32", [128, 4, 32])
    Ktmp = tmp("Ktmp", [128, 128])
    kconst_b = Buf("kconst")
    su_sem = [P.new_sem("setup"), 0]
    su_b = Buf("setup_loads")

    def ld(out, in_, **kw):
        P.dma("sp", out, in_, su_sem, writes=[su_b], **kw)

    pool(lambda e: e.memset(ident[:, :], 0.0), writes=[const_b])
    pool(lambda e: e.affine_select(out=ident[:, :], in_=ident[:, :], compare_op=ALU.not_equal,
                                   fill=1.0, base=0, pattern=[[-1, 128]], channel_multiplier=1),
         writes=[const_b])
    pool(lambda e: e.memset(ones_bf[:, :], 1.0), writes=[const_b])
    pool(lambda e: e.memset(eps_t[:, :], EPS), writes=[const_b])
    m16 = tmp("m16", [128, 8])
    pool(lambda e: e.memset(m16[:, :], 1.0), writes=[const_b])
    pool(lambda e: e.affine_select(out=m16[:, :], in_=m16[:, :], compare_op=ALU.is_ge, fill=0.0,
                                   base=0, pattern=[[-16, 8]], channel_multiplier=1), writes=[const_b])
    pool(lambda e: e.affine_select(out=m16[:, :], in_=m16[:, :], compare_op=ALU.is_ge, fill=0.0,
                                   base=15, pattern=[[16, 8]], channel_multiplier=-1), writes=[const_b])
    pool(lambda e: e.memset(m32[:, :, :], 1.0), writes=[const_b])
    pool(lambda e: e.affine_select(out=m32[:, :, :], in_=m32[:, :, :], compare_op=ALU.is_ge, fill=0.0,
                                   base=0, pattern=[[-32, 4], [0, 32]], channel_multiplier=1), writes=[const_b])
    pool(lambda e: e.affine_select(out=m32[:, :, :], in_=m32[:, :, :], compare_op=ALU.is_ge, fill=0.0,
                                   base=31, pattern=[[32, 4], [0, 32]], channel_multiplier=-1), writes=[const_b])
    ramp_i = tmp("ramp_i", [128, J + 2], I32)
    ramp = tmp("ramp", [128, J + 2])
    pool(lambda e: e.iota(ramp_i[:, :], pattern=[[1, J + 2]], base=0, channel_multiplier=0), writes=[const_b])
    dve(lambda e: e.tensor_copy(ramp[:, :], ramp_i[:, :]), reads=[const_b], writes=[const_b])
    pool(lambda e: e.memset(carry[:, :, :], 0.0), writes=carry_b)
    pool(lambda e: e.memset(UT[:, :, 0:16], 0.0), writes=UT_b)
    pool(lambda e: e.memset(Kblk[:, :, :, :], 0.0), writes=[const_b])
    pool(lambda e: e.memset(W4[:, :, :, :, :], 0.0), writes=[const_b])

    grow = tmp("grow", [64, 128])
    for i, n in enumerate(gain_names):
        ld(grow[8 * i:8 * i + 8, :], gains_d[n].rearrange("(a p) -> a p", p=128))
    for i, n in enumerate(half_names):
        ld(grow[48 + 4 * i:48 + 4 * i + 4, :], halves_d[n].rearrange("(a p) -> a p", p=128))
    lamrow = tmp("lamrow", [16, 2, 128])
    ld(lamrow[:, 0, :], lre_d.rearrange("(b g) n -> b (g n)", g=2))
    ld(lamrow[:, 1, :], lim_d.rearrange("(b g) n -> b (g n)", g=2))
    dt32 = tmp("dt32", [128, 32])
    ld(dt32[:, :], ldt_d.partition_broadcast(128))
    Braw = tmp("Braw", [128, 2, NBLK, 16])
    ld(Braw[:, 0, :, :], bre_d.rearrange("(b g) n h -> (g n) b h", g=2))
    ld(Braw[:, 1, :, :], bim_d.rearrange("(b g) n h -> (g n) b h", g=2))
    Craw = tmp("Craw", [16, 2, NBLK, 128])
    ld(Craw[:, 0, :, :].rearrange("h b (g n) -> h b g n", g=2), cre_d.rearrange("(b g) h n -> h b g n", g=2))
    ld(Craw[:, 1, :, :].rearrange("h b (g n) -> h b g n", g=2), cim_d.rearrange("(b g) h n -> h b g n", g=2))
    gluraw = tmp("gluraw", [128, 4, 32])
    ld(gluraw[:, :, :], wglu_d.rearrange("(c g) h k -> (g h) c k", c=4))
    praw = tmp("praw", [128, 4, 128])
    ld(praw[:, :, :], poolw_d.rearrange("g c d -> c g d"))
    SU = [su_b]

    cv = {}

    def conv(key, out, in_):
        if key not in cv:
            cv[key] = [P.new_sem("cv_" + key), 0, Buf("cv_" + key)]
        s = cv[key]
        sl = [s[0], s[1]]
        P.dma("pool", out, in_, sl, writes=[s[2]])
        s[1] = sl[1]

    def conv_ffn(f):
        gsrc = wg_d[f].rearrange("(dc p) (fc c) -> fc p dc c", p=128, c=128)
        usrc = wu_d[f].rearrange("(dc p) (fc c) -> fc p dc c", p=128, c=128)
        for fc in range(FC):
            conv("g%d" % f, scr_g[f][fc], gsrc[fc])
            conv("u%d" % f, scr_u[f][fc], usrc[fc])
        dsrc = wd_d[f].rearrange("(fc p) (dh c) -> dh fc p c", p=128, c=512)
        for dh in range(2):
            for fc in range(FC):
                conv("d%d" % f, scr_d[f][dh, fc], dsrc[dh, fc])

    conv_ffn(0)
    isrc = win_d.rearrange("(dc p) (oc c) -> oc p dc c", p=128, c=128)
    osrc = wout_d.rearrange("(dc p) (oc c) -> oc p dc c", p=128, c=128)
    for oc in range(8):
        conv("in", scr_in[oc], isrc[oc])
    for oc in range(8):
        conv("out", scr_out[oc], osrc[oc])
    for key in ("g1", "u1", "d1"):
        cv[key] = [P.new_sem("cv_" + key), 0, Buf("cv_" + key)]

    b0 = 4
    pe([tr(ps[:, b0, 0:64], grow[:, :], ident[0:64, 0:64])], reads=SU + [const_b], writes=[ps_b[b0]])
    dve(lambda e, b0=b0: e.tensor_copy(gcol[:, :], ps[:, b0, 0:64]), reads=[ps_b[b0]], writes=[const_b])
    dump("grow", grow[:, :], SU)
    dump("gcol0", gcol[:, :], [const_b])
    dve(lambda e: e.tensor_scalar_mul(gcol[:, 8:16], gcol[:, 8:16], 0.5), reads=[const_b], writes=[const_b])
    dve(lambda e: e.tensor_scalar_mul(gcol[:, 40:48], gcol[:, 40:48], 0.5), reads=[const_b], writes=[const_b])
    G_F1PRE, G_F1POST, G_MPRE, G_MPOST, G_F2PRE, G_F2POST, G_SSMN, G_POOLN, G_PSCALE, G_SSMD = \
        0, 8, 16, 24, 32, 40, 48, 52, 56, 60

    for g, w in enumerate((2, 4, 8, 16)):
        dve(lambda e, g=g, w=w: e.tensor_scalar_mul(Wpool[:, g, 0, :], praw[:, g, :], 1.0 / w - 1.0),
            reads=SU, writes=[const_b])
        dve(lambda e, g=g, w=w: e.tensor_scalar_mul(Wpool[:, g, 1, :], praw[:, g, :], 1.0 / w),
            reads=SU, writes=[const_b])
    for vg in range(2):
        dve(lambda e, vg=vg: e.tensor_tensor(
            Wglu[:, :, vg, :, :],
            gluraw[:, :, 16 * vg:16 * vg + 16].unsqueeze(2).to_broadcast([128, 4, 8, 16]),
            m16[:, :].unsqueeze(1).unsqueeze(3).to_broadcast([128, 4, 8, 16]), ALU.mult),
            reads=SU + [const_b], writes=[const_b])
    dve(lambda e: e.tensor_scalar_mul(Wglu[:, :, 0, :, :], Wglu[:, :, 0, :, :], 0.5), reads=[const_b], writes=[const_b])

    lam = tmp("lam", [128, 2, NBLK])
    b0 = 5
    pe([tr(ps[:, b0, 0:16], lamrow[:, 0, :], ident[0:16, 0:16]),
        tr(ps[:, b0, 16:32], lamrow[:, 1, :], ident[0:16, 0:16])], reads=SU + [const_b], writes=[ps_b[b0]])
    dve(lambda e, b0=b0: e.tensor_copy(lam[:, :, :], ps[:, b0, 0:32].rearrange("p (a b) -> p a b", a=2)),
        reads=[ps_b[b0]], writes=[const_b])
    CT = tmp("CT", [128, 2, NBLK, 16])
    for pl in range(2):
        b0 = 6 + pl
        pe([tr(ps[:, b0, 16 * b:16 * b + 16], Craw[:, pl, b, :], ident[0:16, 0:16]) for b in range(NBLK)],
           reads=SU + [const_b], writes=[ps_b[b0]])
        dve(lambda e, pl=pl, b0=b0: e.tensor_copy(CT[:, pl, :, :], ps[:, b0, 0:256].rearrange("p (b h) -> p b h", h=16)),
            reads=[ps_b[b0]], writes=[const_b])
    dtl = tmp("dtl", [128, NBLK])
    dtv = tmp("dtv", [128, NBLK])
    dt32v = dt32[:, :].rearrange("p (b g) -> p b g", g=2)
    dve(lambda e: e.tensor_copy(dtl[0:64, :], dt32v[0:64, :, 0]), reads=SU, writes=[const_b])
    dve(lambda e: e.tensor_copy(dtl[64:128, :], dt32v[64:128, :, 1]), reads=SU, writes=[const_b])
    act(lambda e: e.activation(dtv[:, :], dtl[:, :], AF.Exp), reads=[const_b], writes=[const_b])
    e1 = tmp("e1", [128, NBLK])
    ang = tmp("ang", [128, NBLK])
    dve(lambda e: e.tensor_tensor(e1[:, :], lam[:, 0, :], dtv[:, :], ALU.mult), reads=[const_b], writes=[const_b])
    dve(lambda e: e.tensor_tensor(ang[:, :], lam[:, 1, :], dtv[:, :], ALU.mult), reads=[const_b], writes=[const_b])

    NM = 5
    scr1 = tmp("scr1", [128, NBLK * (J + 2)])
    scr2 = tmp("scr2", [128, NBLK * (J + 2)])

    def sincos(dst_sin, dst_cos, src, n):
        a1 = scr1[:, 0:n]
        a2 = scr2[:, 0:n]
        for dst, shift in ((dst_sin, 0.0), (dst_cos, math.pi / 2.0)):
            dve(lambda e, shift=shift: e.tensor_scalar(a1, src, shift, 1.0 / TWO_PI, ALU.add, ALU.mult),
                reads=[const_b], writes=[const_b])
            dve(lambda e: e.tensor_scalar(a1, a1, MAGIC, MAGIC, ALU.add, ALU.subtract), reads=[const_b], writes=[const_b])
            dve(lambda e: e.scalar_tensor_tensor(a2, a1, -TWO_PI, src, ALU.mult, ALU.add), reads=[const_b], writes=[const_b])
            dve(lambda e, shift=shift: e.tensor_scalar(a2, a2, shift, math.pi, ALU.add, ALU.min),
                reads=[const_b], writes=[const_b])
            dve(lambda e: e.tensor_scalar_max(a2, a2, -math.pi), reads=[const_b], writes=[const_b])
            act(lambda e, dst=dst: e.activation(dst, a2, AF.Sin), reads=[const_b], writes=[const_b])

    angm = tmp("angm", [128, NM, NBLK])
    magm = tmp("magm", [128, NM, NBLK])
    PW = tmp("PW", [128, 2, NM, NBLK])
    sn = tmp("sn", [128, NM, NBLK])
    cs = tmp("cs", [128, NM, NBLK])
    for m in range(NM):
        dve(lambda e, m=m: e.tensor_scalar_mul(angm[:, m, :], ang[:, :], float(m)), reads=[const_b], writes=[const_b])
        act(lambda e, m=m: e.activation(magm[:, m, :], e1[:, :], AF.Exp, scale=float(m)), reads=[const_b], writes=[const_b])
    fl = lambda t: t[:, :, :].rearrange("p a b -> p (a b)")
    sincos(fl(sn), fl(cs), fl(angm), NM * NBLK)
    dve(lambda e: e.tensor_tensor(PW[:, 0, :, :], magm[:, :, :], cs[:, :, :], ALU.mult), reads=[const_b], writes=[const_b])
    dve(lambda e: e.tensor_tensor(PW[:, 1, :, :], magm[:, :, :], sn[:, :, :], ALU.mult), reads=[const_b], writes=[const_b])
    dve(lambda e: e.tensor_copy(Rdec[:, :], magm[:, 4, :]), reads=[const_b], writes=[const_b])
    th = tmp("th", [128, NBLK])
    dve(lambda e: e.tensor_scalar(scr1[:, 0:NBLK], angm[:, 4, :], 1.0 / TWO_PI, MAGIC, ALU.mult, ALU.add), reads=[const_b], writes=[const_b])
    dve(lambda e: e.tensor_scalar_sub(scr1[:, 0:NBLK], scr1[:, 0:NBLK], MAGIC), reads=[const_b], writes=[const_b])
    dve(lambda e: e.scalar_tensor_tensor(th[:, :], scr1[:, 0:NBLK], -TWO_PI, angm[:, 4, :], ALU.mult, ALU.add), reads=[const_b], writes=[const_b])
    epang = tmp("epang", [128, NBLK, J + 2])
    dve(lambda e: e.tensor_tensor(epang[:, :, :], th[:, :].unsqueeze(2).to_broadcast([128, NBLK, J + 2]),
                                  ramp[:, :].unsqueeze(1).to_broadcast([128, NBLK, J + 2]), ALU.mult),
        reads=[const_b], writes=[const_b])
    NE = NBLK * (J + 2)
    sincos(Ep[:, 1, :, :].rearrange("p b j -> p (b j)"), Ep[:, 0, :, :].rearrange("p b j -> p (b j)"),
           epang[:, :, :].rearrange("p b j -> p (b j)"), NE)

    t16 = tmp("t16", [128, 8, NBLK])
    cb = [const_b]
    ar, ai = PW[:, 0, 1, :], PW[:, 1, 1, :]
    lr_, li_ = lam[:, 0, :], lam[:, 1, :]
    dve(lambda e: e.tensor_scalar_add(t16[:, 0, :], ar, -1.0), reads=cb, writes=cb)
    dve(lambda e: e.tensor_tensor(t16[:, 1, :], lr_, lr_, ALU.mult), reads=cb, writes=cb)
    dve(lambda e: e.tensor_tensor(t16[:, 2, :], li_, li_, ALU.mult), reads=cb, writes=cb)
    dve(lambda e: e.tensor_tensor(t16[:, 1, :], t16[:, 1, :], t16[:, 2, :], ALU.add), reads=cb, writes=cb)
    dve(lambda e: e.reciprocal(t16[:, 1, :], t16[:, 1, :]), reads=cb, writes=cb)
    dve(lambda e: e.tensor_tensor(t16[:, 2, :], t16[:, 0, :], lr_, ALU.mult), reads=cb, writes=cb)
    dve(lambda e: e.tensor_tensor(t16[:, 3, :], ai, li_, ALU.mult), reads=cb, writes=cb)
    dve(lambda e: e.tensor_tensor(t16[:, 2, :], t16[:, 2, :], t16[:, 3, :], ALU.add), reads=cb, writes=cb)
    dve(lambda e: e.tensor_tensor(t16[:, 4, :], t16[:, 2, :], t16[:, 1, :], ALU.mult), reads=cb, writes=cb)
    dve(lambda e: e.tensor_tensor(t16[:, 2, :], ai, lr_, ALU.mult), reads=cb, writes=cb)
    dve(lambda e: e.tensor_tensor(t16[:, 3, :], t16[:, 0, :], li_, ALU.mult), reads=cb, writes=cb)
    dve(lambda e: e.tensor_tensor(t16[:, 2, :], t16[:, 2, :], t16[:, 3, :], ALU.subtract), reads=cb, writes=cb)
    dve(lambda e: e.tensor_tensor(t16[:, 5, :], t16[:, 2, :], t16[:, 1, :], ALU.mult), reads=cb, writes=cb)
    Bbar = tmp("Bbar", [128, 2, NBLK, 16])
    tB = tmp("tB", [128, 4, TC, NBLK, 16])
    bc16 = lambda ap: ap.unsqueeze(2).to_broadcast([128, NBLK, 16])
    qre, qim = t16[:, 4, :], t16[:, 5, :]
    dve(lambda e: e.tensor_tensor(tB[:, 0, 0, :, :], bc16(qre), Braw[:, 0, :, :], ALU.mult), reads=cb + SU, writes=cb)
    dve(lambda e: e.tensor_tensor(tB[:, 1, 0, :, :], bc16(qim), Braw[:, 1, :, :], ALU.mult), reads=cb + SU, writes=cb)
    dve(lambda e: e.tensor_tensor(Bbar[:, 0, :, :], tB[:, 0, 0, :, :], tB[:, 1, 0, :, :], ALU.subtract), reads=cb, writes=cb)
    dve(lambda e: e.tensor_tensor(tB[:, 0, 0, :, :], bc16(qre), Braw[:, 1, :, :], ALU.mult), reads=cb + SU, writes=cb)
    dve(lambda e: e.tensor_tensor(tB[:, 1, 0, :, :], bc16(qim), Braw[:, 0, :, :], ALU.mult), reads=cb + SU, writes=cb)
    dve(lambda e: e.tensor_tensor(Bbar[:, 1, :, :], tB[:, 0, 0, :, :], tB[:, 1, 0, :, :], ALU.add), reads=cb, writes=cb)

    def cprod(dst, pw_lo, X, conj_sign=1.0):
        shp = [128, TC, NBLK, 16]
        pwb = lambda pl: PW[:, pl, pw_lo:pw_lo + TC, :].unsqueeze(3).to_broadcast(shp)
        xb = lambda pl: X[:, pl, :, :].unsqueeze(1).to_broadcast(shp)
        dve(lambda e: e.tensor_tensor(tB[:, 0, :, :, :], pwb(0), xb(0), ALU.mult), reads=cb, writes=cb)
        dve(lambda e: e.tensor_tensor(tB[:, 1, :, :, :], pwb(1), xb(1), ALU.mult), reads=cb, writes=cb)
        dve(lambda e: e.tensor_tensor(tB[:, 2, :, :, :], pwb(0), xb(1), ALU.mult), reads=cb, writes=cb)
        dve(lambda e: e.tensor_tensor(tB[:, 3, :, :, :], pwb(1), xb(0), ALU.mult), reads=cb, writes=cb)
        dve(lambda e: e.tensor_tensor(dst[:, 0, :, :, :], tB[:, 0, :, :, :], tB[:, 1, :, :, :], ALU.subtract), reads=cb, writes=cb)
        dve(lambda e: e.tensor_tensor(dst[:, 1, :, :, :], tB[:, 2, :, :, :], tB[:, 3, :, :, :], ALU.add), reads=cb, writes=cb)

    Dm = tmp("Dm", [128, 2, TC, NBLK, 16])
    Qm = tmp("Qm", [128, 2, TC, NBLK, 16])
    cprod(Dm, 0, Bbar)
    cprod(Qm, 1, CT)
    dve(lambda e: e.memset(ZB[:, :, :, :, :], 0.0), writes=cb)
    dve(lambda e: e.memset(ZC[:, :, :, :], 0.0), writes=cb)
    for g2 in range(2):
        p0, p1, c0 = 64 * g2, 64 * g2 + 64, 16 * g2
        dve(lambda e, p0=p0, p1=p1, c0=c0: e.tensor_copy(ZB[p0:p1, :, :, :, c0:c0 + 16], Dm[p0:p1, :, :, :, :]), reads=cb, writes=cb)
        dve(lambda e, p0=p0, p1=p1, c0=c0: e.tensor_copy(ZC[p0:p1, 0, :, c0:c0 + 16], CT[p0:p1, 0, :, :]), reads=cb, writes=cb)
        dve(lambda e, p0=p0, p1=p1, c0=c0: e.tensor_scalar_mul(ZC[p0:p1, 1, :, c0:c0 + 16], CT[p0:p1, 1, :, :], -1.0), reads=cb, writes=cb)
        dve(lambda e, p0=p0, p1=p1, c0=c0: e.tensor_copy(
            W4[p0:p1, :, :, 0, c0:c0 + 16], Qm[p0:p1, 0, :, :, :].rearrange("p i b h -> p b i h")), reads=cb, writes=cb)
        dve(lambda e, p0=p0, p1=p1, c0=c0: e.tensor_scalar_mul(
            W4[p0:p1, :, :, 1, c0:c0 + 16], Qm[p0:p1, 1, :, :, :].rearrange("p i b h -> p b i h"), -1.0), reads=cb, writes=cb)
    chain_done = [Tok(P.sem[e_], P.cnt[e_]) for e_ in ("pe", "act", "dve", "pool") if P.cnt[e_] > 0]
    kc = [kconst_b]
    rotK = [0]

    def klag_gen():
        for c in range(4):
            for k in range(TC):
                b0 = rotK[0] % 4
                rotK[0] += 1
                fns = []
                for q in range(4):
                    blk = 4 * c + q
                    o = ps[:, b0, 32 * q:32 * q + 32]
                    fns.append(mm(o, ZB[:, 0, k, 4 * c:4 * c + 4, :].rearrange("p b h -> p (b h)"), ZC[:, 0, blk, :], True, False))
                    fns.append(mm(o, ZB[:, 1, k, 4 * c:4 * c + 4, :].rearrange("p b h -> p (b h)"), ZC[:, 1, blk, :], False, True))
                pe(fns, reads=cb, writes=[ps_b[b0]])
                yield
                m32v = m32[:, :, :].rearrange("p q h -> p (q h)")
                if k == 0:
                    dve(lambda e, c=c, b0=b0: e.tensor_tensor(Ktmp[:, :], ps[:, b0, 0:128], m32v, ALU.mult),
                        reads=[ps_b[b0]] + cb, writes=kc)
                    dve(lambda e, c=c: e.scalar_tensor_tensor(Kblk[:, c, 0, :], ident[:, :], gcol[:, G_SSMD + c:G_SSMD + c + 1],
                                                              Ktmp[:, :], ALU.mult, ALU.add), reads=cb, writes=kc)
                else:
                    dve(lambda e, c=c, k=k, b0=b0: e.tensor_tensor(Kblk[:, c, k, :], ps[:, b0, 0:128], m32v, ALU.mult),
                        reads=[ps_b[b0]] + cb, writes=kc)
        for c in range(4):
            for i in range(TC):
                b0 = rotK[0] % 4
                rotK[0] += 1
                pe([tr(ps[:, b0, 128 * pl:128 * pl + 128],
                       ZB[:, pl, TC - 1 - i, 4 * c:4 * c + 4, :].rearrange("p b h -> p (b h)"), ident[:, :]) for pl in range(2)],
                   reads=cb, writes=[ps_b[b0]])
                yield
                dve(lambda e, c=c, i=i, b0=b0: e.tensor_copy(KBT[:, c, i, :, :], ps[:, b0, 0:256].rearrange("p (a n) -> p a n", a=2)),
                    reads=[ps_b[b0]], writes=kc)

    dump("lam", lam[:, :, :], cb)
    dump("CT", CT[:, :, :, :], cb)
    dump("PW", PW[:, :, :, :], cb)
    dump("Ep", Ep[:, :, :, :], cb)
    dump("Rdec", Rdec[:, :], cb)
    dump("Bbar", Bbar[:, :, :, :], cb)
    dump("Kblk", Kblk[:, :, :, :], cb, BF16)
    dump("KBT", KBT[:, :, :, :, :], cb, BF16)
    dump("W4", W4[:, :, :, :, :], cb, BF16)
    dump("Wglu", Wglu[:, :, :, :, :], cb, BF16)
    es.close()
    setup_done = [Tok(P.sem[e_], P.cnt[e_]) for e_ in ("pe", "act", "dve", "pool") if P.cnt[e_] > 0]

    def bl(name, n):
        return [Buf("%s%d" % (name, i)) for i in range(n)]

    _h1 = sb("hT1", [128, DC, T]); _h2 = sb("hT2", [128, DC, T]); _h0 = sb("hT0", [128, DC, T])
    hTs = [_h0, _h1, _h2]
    hTs_b = [bl("hT%d_" % i, DC) for i in range(3)]
    xnF = sb("xnF", [128, DC, T], BF16);    xnF_b = bl("xnF", DC)
    HT = sb("HT", [128, FC, T], BF16);      HT_b = bl("HT", FC)
    fstg = sb("fstg", [128, DC, T], BF16);  fstg_b = bl("fstg", DC)
    sqF = sb("sqF", [128, DC, T], BF16);    sqF_b = bl("sqF", DC)
    rstdF = sb("rstdF", [128, 2, T]);       rstdF_b = bl("rstdF", 2)
    sgtF = sb("sgtF", [128, 2, T]);         sgtF_b = bl("sgtF", 2)
    xnM = sb("xnM", [128, DC, T], BF16);    xnM_b = bl("xnM", DC)
    stg = sb("stg", [128, DC, T], BF16);    stg_b = bl("stg", DC)
    sqM = sb("sqM", [128, DC, T], BF16);    sqM_b = bl("sqM", DC)
    rstdM = sb("rstdM", [128, 2, T]);       rstdM_b = bl("rstdM", 2)
    sgtM = sb("sgtM", [128, 1, T]);         sgtM_b = bl("sgtM", 1)
    xn2 = sb("xn2", [128, DC, T], BF16);    xn2_b = bl("xn2", DC)
    Wst = sb("Wst", [128, 2, 4, J + 2]);   Wst_b = Buf("Wst")
    Cin = sb("Cin", [128, 2, 4, J]);       Cin_b = Buf("Cin")
    Ssb, Ssb_b = Cin, Cin_b
    tw = sb("tw", [128, 4, 4, J + 1]);     tw_b = Buf("tw")
    Xb = sb("Xb", [128, 2, 4, J], BF16);   Xb_b = Buf("Xb")
    NSLOT = 8
    wsl = sb("wsl", [128, NSLOT, 1024], BF16)
    wsl_b = bl("wsl", NSLOT)
    wsl_sem = [[P.new_sem("wsl%d" % i), 0] for i in range(NSLOT)]
    xin = sb("xin", [128, 2, 512]);   xin_b = bl("xin", 2)
    xin_sem = [[P.new_sem("xin%d" % i), 0] for i in range(2)]
    xo = sb("xo", [128, 2, D]);     xo_b = bl("xo", 2)
    xo_sem = [[P.new_sem("xo%d" % i), 0] for i in range(2)]
    for _b in (hTs_b[0] + xnF_b + HT_b + fstg_b + sqF_b + rstdF_b + sgtF_b + xnM_b + stg_b + sqM_b + rstdM_b
               + sgtM_b + xn2_b + [Wst_b, Cin_b, tw_b, Xb_b] + wsl_b + xin_b + xo_b):
        for _i, _t in enumerate(chain_done):
            _b.r["setup%d" % _i] = _t

    slot_ctr = [0]
    rotF = [0]
    rotM = [0]

    def bankF():
        b_ = 4 + (rotF[0] % 4)
        rotF[0] += 1
        return b_

    def bankM():
        b_ = rotM[0] % 4
        rotM[0] += 1
        return b_

    def stream(src_ap, nelem, cvkey):
        s_ = slot_ctr[0] % NSLOT
        slot_ctr[0] += 1
        P.dma("sp", wsl[:, s_, 0:nelem], src_ap, wsl_sem[s_], reads=[cv[cvkey][2]], writes=[wsl_b[s_]])
        return s_

    class Set:
        pass

    SF = Set(); SF.xn, SF.xn_b, SF.sq, SF.sq_b, SF.rstd, SF.rstd_b, SF.bank = xnF, xnF_b, sqF, sqF_b, rstdF, rstdF_b, bankF
    S1 = Set(); S1.xn, S1.xn_b, S1.sq, S1.sq_b, S1.rstd, S1.rstd_b, S1.bank = xnF, xnF_b, sqM, sqM_b, rstdM, rstdM_b, bankM
    S2 = Set(); S2.xn, S2.xn_b, S2.sq, S2.sq_b, S2.rstd, S2.rstd_b, S2.bank = xn2, xn2_b, sqM, sqM_b, rstdM, rstdM_b, bankM
    SM = Set(); SM.xn, SM.xn_b, SM.sq, SM.sq_b, SM.rstd, SM.rstd_b, SM.bank = xnM, xnM_b, sqM, sqM_b, rstdM, rstdM_b, bankM

    def norm_stats(S, chunks, dn, ri):
        b0 = S.bank()
        n = len(chunks)
        pe([mm(ps[:, b0, 0:T], ones_bf[:, :], S.sq[:, c, :], i == 0, i == n - 1) for i, c in enumerate(chunks)],
           reads=[S.sq_b[c] for c in chunks] + [const_b], writes=[ps_b[b0]])
        yield
        act(lambda e: e.activation(S.rstd[:, ri, :], ps[:, b0, 0:T], AF.Ln, bias=eps_t[:, 0:1], scale=1.0 / dn),
            reads=[ps_b[b0], const_b], writes=[S.rstd_b[ri]])
        act(lambda e: e.activation(S.rstd[:, ri, :], S.rstd[:, ri, :], AF.Exp, scale=-0.5), reads=[], writes=[S.rstd_b[ri]])

    KBS = int(os.environ.get("KBS", "2"))

    def pre_norm(S, hT, hT_b, goff):
        for c in range(DC):
            act(lambda e, c=c: e.activation(S.sq[:, c, :], hT[:, c, :], AF.Square), reads=[hT_b[c]], writes=[S.sq_b[c]])
            if c % 2 == 1:
                yield
        yield ("bsteps", 1)
        yield from norm_stats(S, list(range(DC)), float(D), 0)
        yield ("bsteps", KBS)
        for c in range(DC):
            dve(lambda e, c=c: e.scalar_tensor_tensor(S.xn[:, c, :], hT[:, c, :], gcol[:, goff + c:goff + c + 1],
                                                      S.rstd[:, 0, :], ALU.mult, ALU.mult),
                reads=[hT_b[c], S.rstd_b[0], const_b], writes=[S.xn_b[c]])
            if c % 2 == 1:
                yield

    def post_norm_add(S, hT, hT_b, stgX, stgX_b):
        yield from norm_stats(S, list(range(DC)), float(D), 1)
        yield
        for c in range(DC):
            dve(lambda e, c=c: e.tensor_tensor(stgX[:, c, :], stgX[:, c, :], S.rstd[:, 1, :], ALU.mult),
                reads=[S.rstd_b[1]], writes=[stgX_b[c]])
            dve(lambda e, c=c: e.tensor_tensor(hT[:, c, :], hT[:, c, :], stgX[:, c, :], ALU.add),
                reads=[stgX_b[c]], writes=[hT_b[c]])
            yield

    def evac_post(S, b0, c, goff, stgX, stgX_b):
        act(lambda e: e.activation(S.sq[:, c, :], ps[:, b0, 0:T], AF.Square), reads=[ps_b[b0]], writes=[S.sq_b[c]])
        dve(lambda e: e.tensor_scalar_mul(stgX[:, c, :], ps[:, b0, 0:T], gcol[:, goff + c:goff + c + 1]),
            reads=[ps_b[b0], const_b], writes=[stgX_b[c]])

    def ffn(f, gpre, gpost, hT, hT_b, dbgt=-1, Sin=None, mark_gu=None, mark_early=None, post=True):
        S = SF
        KF = 9
        if dbgt == 0:
            dump("rstd0", S.rstd[:, 0, :], [S.rstd_b[0]])
            dump("xn1", S.xn[:, :, :], S.xn_b, BF16)
        for fc in range(FC):
            sg = stream(scr_g[f][fc].rearrange("p a c -> p (a c)"), 1024, "g%d" % f)
            su = stream(scr_u[f][fc].rearrange("p a c -> p (a c)"), 1024, "u%d" % f)
            bg, bu = bankF(), bankF()
            wgv = wsl[:, sg, :].rearrange("p (dc c) -> p dc c", c=128)
            wuv = wsl[:, su, :].rearrange("p (dc c) -> p dc c", c=128)
            pe([mm(ps[:, bg, 0:T], wgv[:, dc, :], Sin.xn[:, dc, :], dc == 0, dc == DC - 1) for dc in range(DC)],
               reads=[wsl_b[sg]] + Sin.xn_b, writes=[ps_b[bg]])
            pe([mm(ps[:, bu, 0:T], wuv[:, dc, :], Sin.xn[:, dc, :], dc == 0, dc == DC - 1) for dc in range(DC)],
               reads=[wsl_b[su]] + Sin.xn_b, writes=[ps_b[bu]])
            si = fc % 2
            act(lambda e, bg=bg, si=si: e.activation(sgtF[:, si, :], ps[:, bg, 0:T], AF.Silu),
                reads=[ps_b[bg]], writes=[sgtF_b[si]])
            dve(lambda e, bu=bu, si=si, fc=fc: e.tensor_tensor(HT[:, fc, :], sgtF[:, si, :], ps[:, bu, 0:T], ALU.mult),
                reads=[sgtF_b[si], ps_b[bu]], writes=[HT_b[fc]])
            if fc == 4 and mark_early is not None:
                yield ("set", mark_early)
            else:
                yield
        if mark_gu is not None:
            yield ("set", mark_gu)
        for dh in range(2):
            banks = [4, 5, 6, 7]
            for fc in range(FC):
                sd = stream(scr_d[f][dh, fc], 512, "d%d" % f)
                for i in range(4):
                    pe([mm(ps[:, banks[i], 0:T], wsl[:, sd, 128 * i:128 * i + 128], HT[:, fc, :], fc == 0, fc == FC - 1)],
                       reads=[wsl_b[sd], HT_b[fc]], writes=[ps_b[banks[i]]])
                if fc % 4 == 3:
                    yield
            for i in range(4):
                evac_post(S, banks[i], 4 * dh + i, gpost, fstg, fstg_b)
            yield
        if post:
            yield from post_norm_add(S, hT, hT_b, fstg, fstg_b)

    def mixer(ti, hT, hT_b):
        S = SM
        xn, xn_b, sq, sq_b, rstd, rstd_b = S.xn, S.xn_b, S.sq, S.sq_b, S.rstd, S.rstd_b
        gl, gl_b = xnM, xnM_b
        yield from pre_norm(S, hT, hT_b, G_MPRE)
        pend = None
        for oc in range(8):
            s_ = stream(scr_in[oc].rearrange("p a c -> p (a c)"), 1024, "in")
            wv = wsl[:, s_, :].rearrange("p (dc c) -> p dc c", c=128)
            b0 = bankM()
            pe([mm(ps[:, b0, 0:T], wv[:, dc, :], xn[:, dc, :], dc == 0, dc == DC - 1) for dc in range(DC)],
               reads=[wsl_b[s_]] + xn_b, writes=[ps_b[b0]])
            if pend is not None:
                pend()
            pend = (lambda b0=b0, oc=oc: act(lambda e: e.activation(UT[:, oc, 16:16 + T], ps[:, b0, 0:T], AF.Copy),
                                           reads=[ps_b[b0]], writes=[UT_b[oc]]))
            yield
        pend()
        yield

        def s_mm(c):
            Uv = UT[:, c, 16:16 + T].rearrange("p (j i) -> p j i", i=TC)
            fns = []
            for q in range(4):
                for pl in range(2):
                    for i in range(TC):
                        kw = {"tile_position": (96, 0)} if q == 3 else {}
                        fns.append(mm(ps[:, q, 256 * pl:256 * pl + J], KBT[32 * q:32 * q + 32, c, i, pl, :],
                                      Uv[32 * q:32 * q + 32, :, i], i == 0, i == TC - 1, **kw))
            pe(fns, reads=[UT_b[c], const_b, kconst_b], writes=[ps_b[q] for q in range(4)])
            yield
            act(lambda e: e.activation(Cin[:, :, :, :].rearrange("p a q j -> p q a j"),
                                       ps[:, 0:4, 0:512].rearrange("p q (a j) -> p q a j", a=2)[:, :, :, 0:J], AF.Copy),
                reads=[ps_b[q] for q in range(4)], writes=[Cin_b])

        def pool_g(g, w):
            oc = 4 + g
            b0 = bankM()
            pe([mm(ps[:, b0, 0:T], Wpool[:, g, 0 if k == 0 else 1, :], UT[:, oc, 16 - k:16 - k + T], k == 0, k == w - 1)
                for k in range(w)], reads=[UT_b[oc], const_b], writes=[ps_b[b0]])
            yield
            act(lambda e: e.activation(stg[:, oc, :], ps[:, b0, 0:T], AF.Copy,
                                       scale=gcol[:, G_PSCALE + oc - 4:G_PSCALE + oc - 3]),
                reads=[ps_b[b0], const_b], writes=[stg_b[oc]])
            act(lambda e: e.activation(sq[:, oc, :], ps[:, b0, 0:T], AF.Square,
                                       scale=gcol[:, G_PSCALE + oc - 4:G_PSCALE + oc - 3]),
                reads=[ps_b[b0], const_b], writes=[sq_b[oc]])
            dve(lambda e: e.tensor_copy(UT[:, oc, 0:16], UT[:, oc, T:T + 16]), reads=[], writes=[UT_b[oc]])

        def ssm_chunk(c):
            Uv = UT[:, c, 16:16 + T].rearrange("p (j i) -> p j i", i=TC)
            yield from s_mm(c)
            yield
            Sre = Cin[:, 0, :, :]
            Sim = Cin[:, 1, :, :]
            blks = slice(4 * c, 4 * c + 4)
            Epr1, Epi1 = Ep[:, 0, blks, 1:J + 1], Ep[:, 1, blks, 1:J + 1]
            sread = [Cin_b, const_b]
            dve(lambda e: e.tensor_tensor(tw[:, 0, :, 0:J], Sre, Epr1, ALU.mult), reads=sread, writes=[tw_b])
            dve(lambda e: e.tensor_tensor(tw[:, 1, :, 0:J], Sim, Epi1, ALU.mult), reads=sread, writes=[tw_b])
            yield
            dve(lambda e: e.tensor_tensor(tw[:, 2, :, 0:J], Sim, Epr1, ALU.mult), reads=sread, writes=[tw_b])
            dve(lambda e: e.tensor_tensor(tw[:, 3, :, 0:J], Sre, Epi1, ALU.mult), reads=sread, writes=[tw_b])
            yield
            dve(lambda e: e.tensor_tensor(Cin[:, 0, :, :], tw[:, 0, :, 0:J], tw[:, 1, :, 0:J], ALU.add), reads=[tw_b], writes=[Cin_b])
            dve(lambda e: e.tensor_tensor(Cin[:, 1, :, :], tw[:, 2, :, 0:J], tw[:, 3, :, 0:J], ALU.subtract), reads=[tw_b], writes=[Cin_b])
            dve(lambda e: e.tensor_copy(Wst[:, :, :, 0], carry[:, :, blks]), reads=[carry_b[c]], writes=[Wst_b])
            yield
            for pl in range(2):
                for q in range(4):
                    blk = 4 * c + q
                    dve(lambda e, pl=pl, q=q, blk=blk: e.tensor_tensor_scan(
                        Wst[:, pl, q, 1:J + 1], Rdec[:, blk:blk + 1].to_broadcast([128, J]), Cin[:, pl, q, :],
                        Wst[:, pl, q, 0:1], ALU.mult, ALU.add), reads=[Cin_b, const_b], writes=[Wst_b])
                    if q % 2 == 1:
                        yield
            Epr0, Epi0 = Ep[:, 0, blks, 0:J + 1], Ep[:, 1, blks, 0:J + 1]
            Wre, Wim = Wst[:, 0, :, 0:J + 1], Wst[:, 1, :, 0:J + 1]
            dve(lambda e: e.tensor_tensor(tw[:, 0, :, :], Wre, Epr0, ALU.mult), reads=[Wst_b, const_b], writes=[tw_b])
            dve(lambda e: e.tensor_tensor(tw[:, 1, :, :], Wim, Epi0, ALU.mult), reads=[Wst_b, const_b], writes=[tw_b])
            yield
            dve(lambda e: e.tensor_tensor(tw[:, 2, :, :], Wim, Epr0, ALU.mult), reads=[Wst_b, const_b], writes=[tw_b])
            dve(lambda e: e.tensor_tensor(tw[:, 3, :, :], Wre, Epi0, ALU.mult), reads=[Wst_b, const_b], writes=[tw_b])
            yield
            dve(lambda e: e.tensor_tensor(Xb[:, 0, :, :], tw[:, 0, :, 0:J], tw[:, 1, :, 0:J], ALU.subtract), reads=[tw_b], writes=[Xb_b])
            dve(lambda e: e.tensor_tensor(Xb[:, 1, :, :], tw[:, 2, :, 0:J], tw[:, 3, :, 0:J], ALU.add), reads=[tw_b], writes=[Xb_b])
            yield
            dve(lambda e: e.tensor_tensor(carry[:, 0, blks], tw[:, 0, :, J], tw[:, 1, :, J], ALU.subtract), reads=[tw_b], writes=[carry_b[c]])
            dve(lambda e: e.tensor_tensor(carry[:, 1, blks], tw[:, 2, :, J], tw[:, 3, :, J], ALU.add), reads=[tw_b], writes=[carry_b[c]])
            yield ("bsteps", KBS)
            by = bankM()
            Yv = ps[:, by, 0:T].rearrange("p (j i) -> p j i", i=TC)
            fns = []
            for k in range(TC):
                fns.append(mm(Yv[:, :, k:TC], Kblk[:, c, k, :], Uv[:, :, 0:TC - k], k == 0, False, skip_group_check=True))
            for q in range(4):
                blk = 4 * c + q
                for i in range(TC):
                    for pl in range(2):
                        last = (q == 3 and i == TC - 1 and pl == 1)
                        kw = {"tile_position": (0, 96)} if q == 3 else {}
                        fns.append(mm(Yv[32 * q:32 * q + 32, :, i], W4[:, blk, i, pl, :], Xb[:, pl, q, :], False, last,
                                      skip_group_check=True, **kw))
            pe(fns, reads=[UT_b[c], Xb_b, const_b, kconst_b], writes=[ps_b[by]])
            yield
            act(lambda e: e.activation(gl[:, c, :], ps[:, by, 0:T], AF.Gelu_apprx_tanh), reads=[ps_b[by]], writes=[gl_b[c]])
            yield
            bv, bg = bankM(), bankM()
            pe([mm(ps[:, bv, 0:T], Wglu[:, c, 0, :, :].rearrange("p g k -> p (g k)"), gl[:, c, :], True, True)],
               reads=[gl_b[c], const_b], writes=[ps_b[bv]])
            pe([mm(ps[:, bg, 0:T], Wglu[:, c, 1, :, :].rearrange("p g k -> p (g k)"), gl[:, c, :], True, True)],
               reads=[gl_b[c], const_b], writes=[ps_b[bg]])
            yield
            si = 0
            act(lambda e: e.activation(sgtM[:, si, :], ps[:, bg, 0:T], AF.Tanh, scale=0.5), reads=[ps_b[bg]], writes=[sgtM_b[si]])
            yield
            dve(lambda e: e.scalar_tensor_tensor(stg[:, c, :], sgtM[:, si, :], 1.0, ps[:, bv, 0:T], ALU.add, ALU.mult),
                reads=[sgtM_b[si], ps_b[bv]], writes=[stg_b[c]])
            act(lambda e: e.activation(sq[:, c, :], stg[:, c, :], AF.Square), reads=[stg_b[c]], writes=[sq_b[c]])
            yield

        for c_ in range(4):
            yield from ssm_chunk(c_)
            yield from pool_g(c_, (2, 4, 8, 16)[c_])
            yield
        if ti == 0:
            dump("UT", UT[:, :, :], UT_b, BF16)
            dump("ymix", stg[:, :, :], stg_b)
        yield from norm_stats(S, [0, 1, 2, 3], 512.0, 0)
        yield from norm_stats(S, [4, 5, 6, 7], 512.0, 1)
        yield
        for c in range(DC):
            ri = 0 if c < 4 else 1
            go = (G_SSMN + c) if c < 4 else (G_POOLN + c - 4)
            dve(lambda e, c=c, ri=ri, go=go: e.scalar_tensor_tensor(xn[:, c, :], stg[:, c, :], gcol[:, go:go + 1],
                                                                    rstd[:, ri, :], ALU.mult, ALU.mult),
                reads=[stg_b[c], rstd_b[ri], const_b], writes=[xn_b[c]])
            if c % 2 == 1:
                yield
        pend = None
        for oc in range(8):
            s_ = stream(scr_out[oc].rearrange("p a c -> p (a c)"), 1024, "out")
            wv = wsl[:, s_, :].rearrange("p (dc c) -> p dc c", c=128)
            b0 = bankM()
            pe([mm(ps[:, b0, 0:T], wv[:, dc, :], xn[:, dc, :], dc == 0, dc == DC - 1) for dc in range(DC)],
               reads=[wsl_b[s_]] + xn_b, writes=[ps_b[b0]])
            if pend is not None:
                pend()
            pend = (lambda b0=b0, oc=oc: evac_post(S, b0, oc, G_MPOST, stg, stg_b))
            yield
        pend()
        yield
        yield from post_norm_add(S, hT, hT_b, stg, stg_b)

    ld_ctr = [0]

    def load_items():
        items = []
        for bi, (a, b) in enumerate(TOKB):
            for hh in range(2):
                items.append((bi, a, b, hh))
        return items

    def load_issue(ti, k):
        bi, a, b, hh = load_items()[k]
        nb = b - a
        s_ = (ld_ctr[0] + k) % 2
        if ti == 0 and bi == 0:
            src = meta_d[0:16, 512 * hh:512 * hh + 512]
        else:
            r0 = ti * T - NMETA + a
            src = x_d[r0:r0 + nb, 512 * hh:512 * hh + 512]
        P.dma("sp", xin[0:nb, s_, :], src, xin_sem[s_], writes=[xin_b[s_]])

    def load_head(ti):
        load_issue(ti, 0)
        load_issue(ti, 1)
        yield

    def load_tile(ti, hT, hT_b, bank, head_done=False, gap=0):
        items = load_items()
        if not head_done:
            load_issue(ti, 0)
            load_issue(ti, 1)
        for k, (bi, a, b, hh) in enumerate(items):
            nb = b - a
            s_ = (ld_ctr[0] + k) % 2
            b0 = bank()
            pe([tr(ps[:, b0, 128 * i:128 * i + nb], xin[0:nb, s_, 128 * i:128 * i + 128], ident[0:nb, 0:nb])
                for i in range(4)], reads=[xin_b[s_], const_b], writes=[ps_b[b0]])
            yield
            dve(lambda e, b0=b0, hh=hh, a=a, b=b, nb=nb: e.tensor_copy(
                hT[:, 4 * hh:4 * hh + 4, a:b], ps[:, b0, :].rearrange("p (i t) -> p i t", t=128)[:, :, 0:nb]),
                reads=[ps_b[b0]], writes=[hT_b[4 * hh + i] for i in range(4)])
            if k + 2 < len(items):
                load_issue(ti, k + 2)
            for _ in range(gap):
                yield
        ld_ctr[0] += len(items)

    def store_tile(ti, hT, hT_b, bank, gap=0):
        for bi, (a, b) in enumerate(TOKB):
            nb = b - a
            if ti == 0 and bi == 0:
                continue
            s_ = bi % 2
            bks = []
            for hh in range(2):
                b0 = bank()
                bks.append(b0)
                pe([tr(ps[0:nb, b0, 128 * i:128 * i + 128], hT[:, 4 * hh + i, a:b], ident[:, :]) for i in range(4)],
                   reads=[hT_b[4 * hh + i] for i in range(4)] + [const_b], writes=[ps_b[b0]])
            yield
            for hh in range(2):
                b0 = bks[hh]
                act(lambda e, b0=b0, hh=hh, nb=nb, s_=s_: e.activation(xo[0:nb, s_, 512 * hh:512 * hh + 512], ps[0:nb, b0, :], AF.Copy),
                    reads=[ps_b[b0]], writes=[xo_b[s_]])
            r0 = ti * T - NMETA + a
            P.dma("pool", out_d[r0:r0 + nb, :], xo[0:nb, s_, :], xo_sem[s_], reads=[xo_b[s_]])
            for _ in range(gap):
                yield

    def run(g):
        for _ in g:
            pass

    def interleave(ga, gb, na, nb_):
        marks = set()
        st = {"a": [ga, 0, True, None], "b": [gb, 0, True, None]}

        def step(k):
            g = st[k]
            if g[3] is not None:
                if g[3] in marks:
                    g[3] = None
                else:
                    return False
            try:
                r = next(g[0])
            except StopIteration:
                g[2] = False
                return True
            g[1] += 1
            if isinstance(r, tuple):
                if r[0] == "set":
                    marks.add(r[1])
                elif r[0] == "wait" and r[1] not in marks:
                    g[3] = r[1]
                elif r[0] == "bsteps" and k == "a":
                    force[0] = r[1]
            return True

        force = [0]
        while st["a"][2] or st["b"][2]:
            pick = "a" if (st["a"][2] and (not st["b"][2] or st["a"][1] * nb_ <= st["b"][1] * na)) else "b"
            if force[0] > 0 and st["b"][2] and st["b"][3] is None:
                pick = "b"
                force[0] -= 1
            if not step(pick):
                other = "b" if pick == "a" else "a"
                assert st[other][2], "interleave deadlock"
                if not step(other):
                    raise RuntimeError("interleave deadlock")
        if os.environ.get("KDBG"):
            print("interleave steps A=%d B=%d" % (st["a"][1], st["b"][1]))

    def chain(*gs):
        for g in gs:
            if isinstance(g, tuple):
                yield g
            else:
                yield from g

    def zipg(g1, g2, n1=1, n2=1):
        gens = [[g1, n1, True], [g2, n2, True]]
        while gens[0][2] or gens[1][2]:
            for g in gens:
                if not g[2]:
                    continue
                for _ in range(g[1]):
                    try:
                        r = next(g[0])
                    except StopIteration:
                        g[2] = False
                        break
                    yield r

    H = lambda t: (hTs[t % 3], hTs_b[t % 3])
    pn1 = lambda t: pre_norm(S1, *H(t), G_F1PRE)
    pn2 = lambda t: pre_norm(S2, *H(t), G_F2PRE)
    f1 = lambda t, **kw: ffn(0, G_F1PRE, G_F1POST, *H(t), Sin=S1, **kw)
    f2 = lambda t, **kw: ffn(1, G_F2PRE, G_F2POST, *H(t), Sin=S2, **kw)
    fpost = lambda t: post_norm_add(SF, *H(t), fstg, fstg_b)
    run(load_tile(0, *H(0), bankF))
    conv_ffn(1)
    run(pn1(0))
    def set_late():
        late_done = [Tok(P.sem[e_], P.cnt[e_]) for e_ in ("pe", "act", "dve", "pool") if P.cnt[e_] > 0]
        for _b in hTs_b[1] + hTs_b[2]:
            for _i, _t in enumerate(late_done):
                _b.r["late%d" % _i] = _t
        yield

    ga = [klag_gen(), set_late()]
    if ntiles > 1:
        ga += [load_tile(1, *H(1), bankM), ("wait", "f1gu"), pn1(1)]
    interleave(chain(*ga), f1(0, mark_gu="f1gu"), 54, 45)
    ga = [mixer(0, *H(0)), pn2(0)]
    if ntiles > 2:
        ga += [load_tile(2, *H(2), bankM), ("wait", "f1gu"), pn1(2)]
    if ntiles > 1:
        interleave(chain(*ga), chain(f1(1, mark_gu="f1gu")), 156, 45)
    else:
        run(chain(*ga))
    pending = None
    for t in range(ntiles):
        gb = []
        if pending is not None:
            gb.append(zipg(chain(pending, ("set", "p1done")), f2(t, mark_early="b_early", post=False), 2, 1))
        else:
            gb += [("set", "p1done"), f2(t, mark_early="b_early", post=False)]
        if t + 2 < ntiles:
            gb.append(zipg(chain(fpost(t), ("set", "f2done")), f1(t + 2, mark_gu="f1gu", mark_early="f1_early", post=False), 2, 1))
            pending = fpost(t + 2)
        else:
            gb += [fpost(t), ("set", "f2done"), ("set", "f1gu"), ("set", "f1_early")]
            pending = None
        ga = [("wait", "b_early")]
        if t + 3 < ntiles:
            ga.append(load_head(t + 3))
        if t + 1 < ntiles:
            ga.append(("wait", "p1done"))
            if os.environ.get("KNOA") != "1":
                ga += [mixer(t + 1, *H(t + 1))]
            ga += [pn2(t + 1)]
        ga += [("wait", "f1_early"), ("wait", "f2done"), store_tile(t, *H(t), bankM, gap=2)]
        if t + 3 < ntiles:
            ga += [load_tile(t + 3, *H(t + 3), bankM, head_done=True, gap=1), ("wait", "f1gu"), pn1(t + 3)]
        interleave(chain(*ga), chain(*gb), 186, 91)
    assert pending is None

    P.wait_only("pool", [t_ for b_ in xo_b for t_ in b_.r.values()])
    if debug:
        P.wait_only("sp", [Tok(dbg_slot[0], dbg_slot[1])])
    P.emit()
    return nc


_NC_CACHE = {}


def kernel(**inputs):
    if "nc" not in _NC_CACHE:
        _NC_CACHE["nc"] = build()
    nc = _NC_CACHE["nc"]
    f32 = lambda a: np.ascontiguousarray(np.asarray(a, dtype=np.float32))
    x = f32(inputs["x"])
    B = x.shape[0]
    shared = {}
    for k, v in inputs.items():
        if k == "x":
            continue
        a = f32(v)
        if k != "meta_tokens":
            a = a[0]
        shared[k] = np.ascontiguousarray(a)
    in_maps = []
    for b in range(B):
        m = dict(shared)
        m["x"] = x[b]
        in_maps.append(m)
    res = run_bass_kernel_spmd(nc, in_maps, core_ids=list(range(B)))
    out = np.stack([np.asarray(r["out"], dtype=np.float32) for r in res.results], axis=0)
    return out
```

```python
import math
import os
import os
import numpy as np
import concourse.bass as bass
import concourse.mybir as mybir
from concourse.bass_utils import run_bass_kernel_spmd

F32 = mybir.dt.float32
BF16 = mybir.dt.bfloat16
I32 = mybir.dt.int32
AF = mybir.ActivationFunctionType
ALU = mybir.AluOpType

D = 1024
DC = 8
FF = 2816
FC = 22
SEQ = 8192
NMETA = 16
LTOT = SEQ + NMETA
T = 432
NT = LTOT // T
TC = 4
J = T // TC
NBLK = 16
EPS = 1e-6
TOKB = [(0, 16), (16, 144), (144, 272), (272, 400), (400, 432)]
TWO_PI = 2.0 * math.pi
MAGIC = 12582912.0
ENGS = ("pe", "act", "dve", "pool", "sp")


class Tok:
    __slots__ = ("sem", "val")

    def __init__(self, sem, val):
        self.sem = sem
        self.val = val


class Buf:
    __slots__ = ("w", "r", "name", "excl")

    def __init__(self, name="", excl=False):
        self.w = None
        self.r = {}
        self.name = name
        self.excl = excl


class Prog:
    def __init__(self, nc):
        self.nc = nc
        self.ops = {e: [] for e in ENGS}
        self.sem = {e: nc.alloc_semaphore("ord_" + e) for e in ENGS}
        self.cnt = {e: 0 for e in ENGS}
        self.waited = {e: {} for e in ENGS}
        self.nsem = 0

    def new_sem(self, name=None):
        self.nsem += 1
        return self.nc.alloc_semaphore(name or ("u%d" % self.nsem))

    def _deps(self, reads, writes, extra):
        deps = list(extra)
        for b in reads:
            if b.w is not None:
                deps.append(b.w)
            if b.excl:
                deps.extend(b.r.values())
        for b in writes:
            if b.w is not None:
                deps.append(b.w)
            deps.extend(b.r.values())
        return deps

    def _waits(self, eng, deps):
        wd = self.waited[eng]
        best = {}
        for d in deps:
            if d is None:
                continue
            if eng == "pe" and d.sem is self.sem["pe"]:
                continue
            k = id(d.sem)
            if k not in best or best[k].val < d.val:
                best[k] = d
        ws = []
        for k, d in best.items():
            if wd.get(k, 0) >= d.val:
                continue
            wd[k] = d.val
            ws.append((d.sem, d.val))
        return ws

    def _mark(self, eng, tok, reads, writes):
        for b in reads:
            b.r[eng + str(id(tok.sem))] = tok
        for b in writes:
            b.w = tok
            b.r = {}

    def op(self, eng, fn, reads=(), writes=(), extra=()):
        ws = self._waits(eng, self._deps(reads, writes, extra))
        self.cnt[eng] += 1
        tok = Tok(self.sem[eng], self.cnt[eng])
        self.ops[eng].append((fn, ws, (self.sem[eng], 1)))
        self._mark(eng, tok, reads, writes)
        return tok

    def group(self, eng, fns, reads=(), writes=(), extra=()):
        ws = self._waits(eng, self._deps(reads, writes, extra))
        self.cnt[eng] += 1
        tok = Tok(self.sem[eng], self.cnt[eng])
        n = len(fns)
        for i, fn in enumerate(fns):
            self.ops[eng].append((fn, ws if i == 0 else [], (self.sem[eng], 1) if i == n - 1 else None))
        self._mark(eng, tok, reads, writes)
        return tok

    def dma(self, eng, out, in_, slot, reads=(), writes=(), extra=(), **kw):
        ws = self._waits(eng, self._deps(reads, writes, extra))
        slot[1] += 16
        tok = Tok(slot[0], slot[1])
        self.ops[eng].append((lambda e: e.dma_start(out=out, in_=in_, **kw), ws, (slot[0], 16)))
        self._mark(eng, tok, reads, writes)
        return tok

    def wait_only(self, eng, deps):
        ws = self._waits(eng, deps)
        if ws:
            self.ops[eng].append((None, ws, None))

    def emit(self):
        nc = self.nc
        with nc.Block() as block:
            def run(e, name):
                for fn, ws, inc in self.ops[name]:
                    for (s, v) in ws:
                        e.wait_ge(s, v)
                    if fn is None:
                        continue
                    ins = fn(e)
                    if inc is not None:
                        ins.then_inc(inc[0], inc[1])

            @block.tensor
            def _(e):
                run(e, "pe")

            @block.scalar
            def _(e):
                run(e, "act")

            @block.vector
            def _(e):
                run(e, "dve")

            @block.gpsimd
            def _(e):
                run(e, "pool")

            @block.sync
            def _(e):
                run(e, "sp")


def build(ntiles=NT, debug=False):
    nc = bass.Bass("TRN2", target_bir_lowering=False)
    P = Prog(nc)

    def din(name, shape):
        return nc.dram_tensor(name, list(shape), F32, kind="ExternalInput").ap()

    x_d = din("x", [SEQ, D])
    meta_d = din("meta_tokens", [NMETA, D])
    gain_names = ["ffn1_pre_norm", "ffn1_post_norm", "mix_pre_norm", "mix_post_norm",
                  "ffn2_pre_norm", "ffn2_post_norm"]
    gains_d = {n: din(n, [D]) for n in gain_names}
    half_names = ["ssm_out_norm", "pool_out_norm", "pool_scale", "ssm_d"]
    halves_d = {n: din(n, [512]) for n in half_names}
    wg_d = [din("ffn1_w_gate", [D, FF]), din("ffn2_w_gate", [D, FF])]
    wu_d = [din("ffn1_w_up", [D, FF]), din("ffn2_w_up", [D, FF])]
    wd_d = [din("ffn1_w_down", [FF, D]), din("ffn2_w_down", [FF, D])]
    win_d = din("w_in", [D, D])
    wout_d = din("w_out", [D, D])
    lre_d = din("ssm_lambda_re", [32, 64])
    lim_d = din("ssm_lambda_im", [32, 64])
    ldt_d = din("ssm_log_dt", [32])
    bre_d = din("ssm_b_re", [32, 64, 16])
    bim_d = din("ssm_b_im", [32, 64, 16])
    cre_d = din("ssm_c_re", [32, 16, 64])
    cim_d = din("ssm_c_im", [32, 16, 64])
    wglu_d = din("ssm_w_glu", [32, 16, 32])
    poolw_d = din("pool_w", [4, 128, 128])
    out_d = nc.dram_tensor("out", [SEQ, D], F32, kind="ExternalOutput").ap()

    def dscr(name, shape):
        return nc.dram_tensor(name, list(shape), BF16, kind="Internal").ap()

    scr_g = [dscr("scr_g%d" % f, [FC, 128, 8, 128]) for f in range(2)]
    scr_u = [dscr("scr_u%d" % f, [FC, 128, 8, 128]) for f in range(2)]
    scr_d = [dscr("scr_d%d" % f, [2, FC, 128, 512]) for f in range(2)]
    scr_in = dscr("scr_in", [8, 128, 8, 128])
    scr_out = dscr("scr_out", [8, 128, 8, 128])

    def sb(name, shape, dt=F32):
        return nc.alloc_sbuf_tensor(name, list(shape), dt)

    Ep = sb("Ep", [128, 2, NBLK, J + 2]);  Ep_b = Buf("Ep")
    Rdec = sb("Rdec", [128, NBLK])
    carry = sb("carry", [128, 2, NBLK]);   carry_b = [Buf("carry%d" % c) for c in range(4)]
    UT = sb("UT", [128, DC, 16 + T], BF16);  UT_b = [Buf("UT%d" % i) for i in range(DC)]
    eps_t = sb("eps_t", [128, 1])
    Kblk = sb("Kblk", [128, 4, TC, 128], BF16)
    KBT = sb("KBT", [128, 4, TC, 2, 128], BF16)
    W4 = sb("W4", [128, NBLK, TC, 2, 32], BF16)
    Wglu = sb("Wglu", [128, 4, 2, 8, 16], BF16)
    Wpool = sb("Wpool", [128, 4, 2, 128], BF16)
    gcol = sb("gcol", [128, 64])
    ident = sb("ident", [128, 128])
    ones_bf = sb("ones_bf", [128, 128], BF16)
    const_b = Buf("const")

    ps = nc.alloc_psum_tensor("ps", [128, 8, 512], F32)
    ps_b = [Buf("ps%d" % i, excl=True) for i in range(8)]
    rot = [0]

    def nbank():
        b = 4 + (rot[0] % 4)
        rot[0] += 1
        return b

    def act(fn, reads=(), writes=(), extra=()):
        return P.op("act", fn, reads, writes, extra)

    def dve(fn, reads=(), writes=(), extra=()):
        return P.op("dve", fn, reads, writes, extra)

    def pool(fn, reads=(), writes=(), extra=()):
        return P.op("pool", fn, reads, writes, extra)

    def pe(fns, reads=(), writes=(), extra=()):
        return P.group("pe", fns, reads, writes, extra)

    def mm(out, lhsT, rhs, start, stop, **kw):
        return lambda e: e.matmul(out, lhsT, rhs, start=start, stop=stop, **kw)

    def tr(out, in_, idn):
        return lambda e: e.transpose(out, in_, idn)

    from contextlib import ExitStack
    es = ExitStack()

    def tmp(name, shape, dt=F32):
        return es.enter_context(nc.sbuf_tensor(name, list(shape), dt))

    dbg_slot = [P.new_sem("dbg"), 0]

    def dump(name, ap, bufs, dt=F32):
        if not debug:
            return
        dd = nc.dram_tensor("dbg_" + name, list(ap.shape), dt, kind="ExternalOutput").ap()
        P.dma("sp", dd, ap, dbg_slot, reads=bufs)
    ZB = tmp("ZB", [128, 2, TC, NBLK, 32])
    ZC = tmp("ZC", [128, 2, NBLK, 32])
    m32 = tmp("m32", [128, 4, 32])
    Ktmp = tmp("Ktmp", [128, 128])
    kconst_b = Buf("kconst")
    su_sem = [P.new_sem("setup"), 0]
    su_b = Buf("setup_loads")

    def ld(out, in_, **kw):
        P.dma("sp", out, in_, su_sem, writes=[su_b], **kw)

    pool(lambda e: e.memset(ident[:, :], 0.0), writes=[const_b])
    pool(lambda e: e.affine_select(out=ident[:, :], in_=ident[:, :], compare_op=ALU.not_equal,
                                   fill=1.0, base=0, pattern=[[-1, 128]], channel_multiplier=1),
         writes=[const_b])
    pool(lambda e: e.memset(ones_bf[:, :], 1.0), writes=[const_b])
    pool(lambda e: e.memset(eps_t[:, :], EPS), writes=[const_b])
    m16 = tmp("m16", [128, 8])
    pool(lambda e: e.memset(m16[:, :], 1.0), writes=[const_b])
    pool(lambda e: e.affine_select(out=m16[:, :], in_=m16[:, :], compare_op=ALU.is_ge, fill=0.0,
                                   base=0, pattern=[[-16, 8]], channel_multiplier=1), writes=[const_b])
    pool(lambda e: e.affine_select(out=m16[:, :], in_=m16[:, :], compare_op=ALU.is_ge, fill=0.0,
                                   base=15, pattern=[[16, 8]], channel_multiplier=-1), writes=[const_b])
    pool(lambda e: e.memset(m32[:, :, :], 1.0), writes=[const_b])
    pool(lambda e: e.affine_select(out=m32[:, :, :], in_=m32[:, :, :], compare_op=ALU.is_ge, fill=0.0,
                                   base=0, pattern=[[-32, 4], [0, 32]], channel_multiplier=1), writes=[const_b])
    pool(lambda e: e.affine_select(out=m32[:, :, :], in_=m32[:, :, :], compare_op=ALU.is_ge, fill=0.0,
                                   base=31, pattern=[[32, 4], [0, 32]], channel_multiplier=-1), writes=[const_b])
    ramp_i = tmp("ramp_i", [128, J + 2], I32)
    ramp = tmp("ramp", [128, J + 2])
    pool(lambda e: e.iota(ramp_i[:, :], pattern=[[1, J + 2]], base=0, channel_multiplier=0), writes=[const_b])
    dve(lambda e: e.tensor_copy(ramp[:, :], ramp_i[:, :]), reads=[const_b], writes=[const_b])
    pool(lambda e: e.memset(carry[:, :, :], 0.0), writes=carry_b)
    pool(lambda e: e.memset(UT[:, :, 0:16], 0.0), writes=UT_b)
    dve(lambda e: e.memset(W4[:, :, :, :, :], 0.0), writes=[const_b])

    grow = tmp("grow", [64, 128])
    for i, n in enumerate(gain_names):
        ld(grow[8 * i:8 * i + 8, :], gains_d[n].rearrange("(a p) -> a p", p=128))
    for i, n in enumerate(half_names):
        ld(grow[48 + 4 * i:48 + 4 * i + 4, :], halves_d[n].rearrange("(a p) -> a p", p=128))
    lamrow = tmp("lamrow", [16, 2, 128])
    ld(lamrow[:, 0, :], lre_d.rearrange("(b g) n -> b (g n)", g=2))
    ld(lamrow[:, 1, :], lim_d.rearrange("(b g) n -> b (g n)", g=2))
    dt32 = tmp("dt32", [128, 32])
    ld(dt32[:, :], ldt_d.partition_broadcast(128))
    Braw = tmp("Braw", [128, 2, NBLK, 16])
    ld(Braw[:, 0, :, :], bre_d.rearrange("(b g) n h -> (g n) b h", g=2))
    ld(Braw[:, 1, :, :], bim_d.rearrange("(b g) n h -> (g n) b h", g=2))
    Craw = tmp("Craw", [16, 2, NBLK, 128])
    ld(Craw[:, 0, :, :].rearrange("h b (g n) -> h b g n", g=2), cre_d.rearrange("(b g) h n -> h b g n", g=2))
    ld(Craw[:, 1, :, :].rearrange("h b (g n) -> h b g n", g=2), cim_d.rearrange("(b g) h n -> h b g n", g=2))
    gluraw = tmp("gluraw", [128, 4, 32])
    ld(gluraw[:, :, :], wglu_d.rearrange("(c g) h k -> (g h) c k", c=4))
    praw = tmp("praw", [128, 4, 128])
    ld(praw[:, :, :], poolw_d.rearrange("g c d -> c g d"))
    SU = [su_b]

    cv = {}

    def conv(key, out, in_):
        if key not in cv:
            cv[key] = [P.new_sem("cv_" + key), 0, Buf("cv_" + key)]
        s = cv[key]
        sl = [s[0], s[1]]
        P.dma("pool", out, in_, sl, writes=[s[2]])
        s[1] = sl[1]

    def conv_ffn(f):
        gsrc = wg_d[f].rearrange("(dc p) (fc c) -> fc p dc c", p=128, c=128)
        usrc = wu_d[f].rearrange("(dc p) (fc c) -> fc p dc c", p=128, c=128)
        for fc in range(FC):
            conv("g%d" % f, scr_g[f][fc], gsrc[fc])
            conv("u%d" % f, scr_u[f][fc], usrc[fc])
        dsrc = wd_d[f].rearrange("(fc p) (dh c) -> dh fc p c", p=128, c=512)
        for dh in range(2):
            for fc in range(FC):
                conv("d%d" % f, scr_d[f][dh, fc], dsrc[dh, fc])

    conv_ffn(0)
    isrc = win_d.rearrange("(dc p) (oc c) -> oc p dc c", p=128, c=128)
    osrc = wout_d.rearrange("(dc p) (oc c) -> oc p dc c", p=128, c=128)
    for oc in range(8):
        conv("in", scr_in[oc], isrc[oc])
    for oc in range(8):
        conv("out", scr_out[oc], osrc[oc])
    for key in ("g1", "u1", "d1"):
        cv[key] = [P.new_sem("cv_" + key), 0, Buf("cv_" + key)]

    b0 = 4
    pe([tr(ps[:, b0, 0:64], grow[:, :], ident[0:64, 0:64])], reads=SU + [const_b], writes=[ps_b[b0]])
    dve(lambda e, b0=b0: e.tensor_copy(gcol[:, :], ps[:, b0, 0:64]), reads=[ps_b[b0]], writes=[const_b])
    dump("grow", grow[:, :], SU)
    dump("gcol0", gcol[:, :], [const_b])
    dve(lambda e: e.tensor_scalar_mul(gcol[:, 8:16], gcol[:, 8:16], 0.5), reads=[const_b], writes=[const_b])
    dve(lambda e: e.tensor_scalar_mul(gcol[:, 40:48], gcol[:, 40:48], 0.5), reads=[const_b], writes=[const_b])
    G_F1PRE, G_F1POST, G_MPRE, G_MPOST, G_F2PRE, G_F2POST, G_SSMN, G_POOLN, G_PSCALE, G_SSMD = \
        0, 8, 16, 24, 32, 40, 48, 52, 56, 60

    for g, w in enumerate((2, 4, 8, 16)):
        dve(lambda e, g=g, w=w: e.tensor_scalar_mul(Wpool[:, g, 0, :], praw[:, g, :], 1.0 / w - 1.0),
            reads=SU, writes=[const_b])
        dve(lambda e, g=g, w=w: e.tensor_scalar_mul(Wpool[:, g, 1, :], praw[:, g, :], 1.0 / w),
            reads=SU, writes=[const_b])
    for vg in range(2):
        dve(lambda e, vg=vg: e.tensor_tensor(
            Wglu[:, :, vg, :, :],
            gluraw[:, :, 16 * vg:16 * vg + 16].unsqueeze(2).to_broadcast([128, 4, 8, 16]),
            m16[:, :].unsqueeze(1).unsqueeze(3).to_broadcast([128, 4, 8, 16]), ALU.mult),
            reads=SU + [const_b], writes=[const_b])
    dve(lambda e: e.tensor_scalar_mul(Wglu[:, :, 0, :, :], Wglu[:, :, 0, :, :], 0.5), reads=[const_b], writes=[const_b])

    lam = tmp("lam", [128, 2, NBLK])
    b0 = 5
    pe([tr(ps[:, b0, 0:16], lamrow[:, 0, :], ident[0:16, 0:16]),
        tr(ps[:, b0, 16:32], lamrow[:, 1, :], ident[0:16, 0:16])], reads=SU + [const_b], writes=[ps_b[b0]])
    dve(lambda e, b0=b0: e.tensor_copy(lam[:, :, :], ps[:, b0, 0:32].rearrange("p (a b) -> p a b", a=2)),
        reads=[ps_b[b0]], writes=[const_b])
    CT = tmp("CT", [128, 2, NBLK, 16])
    for pl in range(2):
        b0 = 6 + pl
        pe([tr(ps[:, b0, 16 * b:16 * b + 16], Craw[:, pl, b, :], ident[0:16, 0:16]) for b in range(NBLK)],
           reads=SU + [const_b], writes=[ps_b[b0]])
        dve(lambda e, pl=pl, b0=b0: e.tensor_copy(CT[:, pl, :, :], ps[:, b0, 0:256].rearrange("p (b h) -> p b h", h=16)),
            reads=[ps_b[b0]], writes=[const_b])
    dtl = tmp("dtl", [128, NBLK])
    dtv = tmp("dtv", [128, NBLK])
    dt32v = dt32[:, :].rearrange("p (b g) -> p b g", g=2)
    dve(lambda e: e.tensor_copy(dtl[0:64, :], dt32v[0:64, :, 0]), reads=SU, writes=[const_b])
    dve(lambda e: e.tensor_copy(dtl[64:128, :], dt32v[64:128, :, 1]), reads=SU, writes=[const_b])
    act(lambda e: e.activation(dtv[:, :], dtl[:, :], AF.Exp), reads=[const_b], writes=[const_b])
    e1 = tmp("e1", [128, NBLK])
    ang = tmp("ang", [128, NBLK])
    dve(lambda e: e.tensor_tensor(e1[:, :], lam[:, 0, :], dtv[:, :], ALU.mult), reads=[const_b], writes=[const_b])
    dve(lambda e: e.tensor_tensor(ang[:, :], lam[:, 1, :], dtv[:, :], ALU.mult), reads=[const_b], writes=[const_b])

    NM = 5
    scr1 = tmp("scr1", [128, NBLK * (J + 2)])
    scr2 = tmp("scr2", [128, NBLK * (J + 2)])

    def sincos(dst_sin, dst_cos, src, n):
        a1 = scr1[:, 0:n]
        a2 = scr2[:, 0:n]
        for dst, shift in ((dst_sin, 0.0), (dst_cos, math.pi / 2.0)):
            dve(lambda e, shift=shift: e.tensor_scalar(a1, src, shift, 1.0 / TWO_PI, ALU.add, ALU.mult),
                reads=[const_b], writes=[const_b])
            dve(lambda e: e.tensor_scalar(a1, a1, MAGIC, MAGIC, ALU.add, ALU.subtract), reads=[const_b], writes=[const_b])
            dve(lambda e: e.scalar_tensor_tensor(a2, a1, -TWO_PI, src, ALU.mult, ALU.add), reads=[const_b], writes=[const_b])
            dve(lambda e, shift=shift: e.tensor_scalar(a2, a2, shift, math.pi, ALU.add, ALU.min),
                reads=[const_b], writes=[const_b])
            dve(lambda e: e.tensor_scalar_max(a2, a2, -math.pi), reads=[const_b], writes=[const_b])
            act(lambda e, dst=dst: e.activation(dst, a2, AF.Sin), reads=[const_b], writes=[const_b])

    angm = tmp("angm", [128, NM, NBLK])
    magm = tmp("magm", [128, NM, NBLK])
    PW = tmp("PW", [128, 2, NM, NBLK])
    sn = tmp("sn", [128, NM, NBLK])
    cs = tmp("cs", [128, NM, NBLK])
    for m in range(NM):
        dve(lambda e, m=m: e.tensor_scalar_mul(angm[:, m, :], ang[:, :], float(m)), reads=[const_b], writes=[const_b])
        act(lambda e, m=m: e.activation(magm[:, m, :], e1[:, :], AF.Exp, scale=float(m)), reads=[const_b], writes=[const_b])
    fl = lambda t: t[:, :, :].rearrange("p a b -> p (a b)")
    sincos(fl(sn), fl(cs), fl(angm), NM * NBLK)
    dve(lambda e: e.tensor_tensor(PW[:, 0, :, :], magm[:, :, :], cs[:, :, :], ALU.mult), reads=[const_b], writes=[const_b])
    dve(lambda e: e.tensor_tensor(PW[:, 1, :, :], magm[:, :, :], sn[:, :, :], ALU.mult), reads=[const_b], writes=[const_b])
    dve(lambda e: e.tensor_copy(Rdec[:, :], magm[:, 4, :]), reads=[const_b], writes=[const_b])
    th = tmp("th", [128, NBLK])
    dve(lambda e: e.tensor_scalar(scr1[:, 0:NBLK], angm[:, 4, :], 1.0 / TWO_PI, MAGIC, ALU.mult, ALU.add), reads=[const_b], writes=[const_b])
    dve(lambda e: e.tensor_scalar_sub(scr1[:, 0:NBLK], scr1[:, 0:NBLK], MAGIC), reads=[const_b], writes=[const_b])
    dve(lambda e: e.scalar_tensor_tensor(th[:, :], scr1[:, 0:NBLK], -TWO_PI, angm[:, 4, :], ALU.mult, ALU.add), reads=[const_b], writes=[const_b])
    epang = tmp("epang", [128, NBLK, J + 2])
    dve(lambda e: e.tensor_tensor(epang[:, :, :], th[:, :].unsqueeze(2).to_broadcast([128, NBLK, J + 2]),
                                  ramp[:, :].unsqueeze(1).to_broadcast([128, NBLK, J + 2]), ALU.mult),
        reads=[const_b], writes=[const_b])
    NE = NBLK * (J + 2)
    sincos(Ep[:, 1, :, :].rearrange("p b j -> p (b j)"), Ep[:, 0, :, :].rearrange("p b j -> p (b j)"),
           epang[:, :, :].rearrange("p b j -> p (b j)"), NE)

    t16 = tmp("t16", [128, 8, NBLK])
    cb = [const_b]
    ar, ai = PW[:, 0, 1, :], PW[:, 1, 1, :]
    lr_, li_ = lam[:, 0, :], lam[:, 1, :]
    dve(lambda e: e.tensor_scalar_add(t16[:, 0, :], ar, -1.0), reads=cb, writes=cb)
    dve(lambda e: e.tensor_tensor(t16[:, 1, :], lr_, lr_, ALU.mult), reads=cb, writes=cb)
    dve(lambda e: e.tensor_tensor(t16[:, 2, :], li_, li_, ALU.mult), reads=cb, writes=cb)
    dve(lambda e: e.tensor_tensor(t16[:, 1, :], t16[:, 1, :], t16[:, 2, :], ALU.add), reads=cb, writes=cb)
    dve(lambda e: e.reciprocal(t16[:, 1, :], t16[:, 1, :]), reads=cb, writes=cb)
    dve(lambda e: e.tensor_tensor(t16[:, 2, :], t16[:, 0, :], lr_, ALU.mult), reads=cb, writes=cb)
    dve(lambda e: e.tensor_tensor(t16[:, 3, :], ai, li_, ALU.mult), reads=cb, writes=cb)
    dve(lambda e: e.tensor_tensor(t16[:, 2, :], t16[:, 2, :], t16[:, 3, :], ALU.add), reads=cb, writes=cb)
    dve(lambda e: e.tensor_tensor(t16[:, 4, :], t16[:, 2, :], t16[:, 1, :], ALU.mult), reads=cb, writes=cb)
    dve(lambda e: e.tensor_tensor(t16[:, 2, :], ai, lr_, ALU.mult), reads=cb, writes=cb)
    dve(lambda e: e.tensor_tensor(t16[:, 3, :], t16[:, 0, :], li_, ALU.mult), reads=cb, writes=cb)
    dve(lambda e: e.tensor_tensor(t16[:, 2, :], t16[:, 2, :], t16[:, 3, :], ALU.subtract), reads=cb, writes=cb)
    dve(lambda e: e.tensor_tensor(t16[:, 5, :], t16[:, 2, :], t16[:, 1, :], ALU.mult), reads=cb, writes=cb)
    Bbar = tmp("Bbar", [128, 2, NBLK, 16])
    tB = tmp("tB", [128, 4, TC, NBLK, 16])
    bc16 = lambda ap: ap.unsqueeze(2).to_broadcast([128, NBLK, 16])
    qre, qim = t16[:, 4, :], t16[:, 5, :]
    dve(lambda e: e.tensor_tensor(tB[:, 0, 0, :, :], bc16(qre), Braw[:, 0, :, :], ALU.mult), reads=cb + SU, writes=cb)
    dve(lambda e: e.tensor_tensor(tB[:, 1, 0, :, :], bc16(qim), Braw[:, 1, :, :], ALU.mult), reads=cb + SU, writes=cb)
    dve(lambda e: e.tensor_tensor(Bbar[:, 0, :, :], tB[:, 0, 0, :, :], tB[:, 1, 0, :, :], ALU.subtract), reads=cb, writes=cb)
    dve(lambda e: e.tensor_tensor(tB[:, 0, 0, :, :], bc16(qre), Braw[:, 1, :, :], ALU.mult), reads=cb + SU, writes=cb)
    dve(lambda e: e.tensor_tensor(tB[:, 1, 0, :, :], bc16(qim), Braw[:, 0, :, :], ALU.mult), reads=cb + SU, writes=cb)
    dve(lambda e: e.tensor_tensor(Bbar[:, 1, :, :], tB[:, 0, 0, :, :], tB[:, 1, 0, :, :], ALU.add), reads=cb, writes=cb)

    def cprod(dst, pw_lo, X, conj_sign=1.0):
        shp = [128, TC, NBLK, 16]
        pwb = lambda pl: PW[:, pl, pw_lo:pw_lo + TC, :].unsqueeze(3).to_broadcast(shp)
        xb = lambda pl: X[:, pl, :, :].unsqueeze(1).to_broadcast(shp)
        dve(lambda e: e.tensor_tensor(tB[:, 0, :, :, :], pwb(0), xb(0), ALU.mult), reads=cb, writes=cb)
        dve(lambda e: e.tensor_tensor(tB[:, 1, :, :, :], pwb(1), xb(1), ALU.mult), reads=cb, writes=cb)
        dve(lambda e: e.tensor_tensor(tB[:, 2, :, :, :], pwb(0), xb(1), ALU.mult), reads=cb, writes=cb)
        dve(lambda e: e.tensor_tensor(tB[:, 3, :, :, :], pwb(1), xb(0), ALU.mult), reads=cb, writes=cb)
        dve(lambda e: e.tensor_tensor(dst[:, 0, :, :, :], tB[:, 0, :, :, :], tB[:, 1, :, :, :], ALU.subtract), reads=cb, writes=cb)
        dve(lambda e: e.tensor_tensor(dst[:, 1, :, :, :], tB[:, 2, :, :, :], tB[:, 3, :, :, :], ALU.add), reads=cb, writes=cb)

    Dm = tmp("Dm", [128, 2, TC, NBLK, 16])
    Qm = tmp("Qm", [128, 2, TC, NBLK, 16])
    cprod(Dm, 0, Bbar)
    cprod(Qm, 1, CT)
    dve(lambda e: e.memset(ZB[:, :, :, :, :], 0.0), writes=cb)
    dve(lambda e: e.memset(ZC[:, :, :, :], 0.0), writes=cb)
    for g2 in range(2):
        p0, p1, c0 = 64 * g2, 64 * g2 + 64, 16 * g2
        dve(lambda e, p0=p0, p1=p1, c0=c0: e.tensor_copy(ZB[p0:p1, :, :, :, c0:c0 + 16], Dm[p0:p1, :, :, :, :]), reads=cb, writes=cb)
        dve(lambda e, p0=p0, p1=p1, c0=c0: e.tensor_copy(ZC[p0:p1, 0, :, c0:c0 + 16], CT[p0:p1, 0, :, :]), reads=cb, writes=cb)
        dve(lambda e, p0=p0, p1=p1, c0=c0: e.tensor_scalar_mul(ZC[p0:p1, 1, :, c0:c0 + 16], CT[p0:p1, 1, :, :], -1.0), reads=cb, writes=cb)
        dve(lambda e, p0=p0, p1=p1, c0=c0: e.tensor_copy(
            W4[p0:p1, :, :, 0, c0:c0 + 16], Qm[p0:p1, 0, :, :, :].rearrange("p i b h -> p b i h")), reads=cb, writes=cb)
        dve(lambda e, p0=p0, p1=p1, c0=c0: e.tensor_scalar_mul(
            W4[p0:p1, :, :, 1, c0:c0 + 16], Qm[p0:p1, 1, :, :, :].rearrange("p i b h -> p b i h"), -1.0), reads=cb, writes=cb)
    chain_done = [Tok(P.sem[e_], P.cnt[e_]) for e_ in ("pe", "act", "dve", "pool") if P.cnt[e_] > 0]
    kc = [kconst_b]
    rotK = [0]

    def klag_gen():
        for c in range(4):
            for k in range(TC):
                b0 = rotK[0] % 4
                rotK[0] += 1
                fns = []
                for q in range(4):
                    blk = 4 * c + q
                    o = ps[:, b0, 32 * q:32 * q + 32]
                    fns.append(mm(o, ZB[:, 0, k, 4 * c:4 * c + 4, :].rearrange("p b h -> p (b h)"), ZC[:, 0, blk, :], True, False))
                    fns.append(mm(o, ZB[:, 1, k, 4 * c:4 * c + 4, :].rearrange("p b h -> p (b h)"), ZC[:, 1, blk, :], False, True))
                pe(fns, reads=cb, writes=[ps_b[b0]])
                yield
                m32v = m32[:, :, :].rearrange("p q h -> p (q h)")
                if k == 0:
                    dve(lambda e, c=c, b0=b0: e.tensor_tensor(Ktmp[:, :], ps[:, b0, 0:128], m32v, ALU.mult),
                        reads=[ps_b[b0]] + cb, writes=kc)
                    dve(lambda e, c=c: e.scalar_tensor_tensor(Kblk[:, c, 0, :], ident[:, :], gcol[:, G_SSMD + c:G_SSMD + c + 1],
                                                              Ktmp[:, :], ALU.mult, ALU.add), reads=cb, writes=kc)
                else:
                    dve(lambda e, c=c, k=k, b0=b0: e.tensor_tensor(Kblk[:, c, k, :], ps[:, b0, 0:128], m32v, ALU.mult),
                        reads=[ps_b[b0]] + cb, writes=kc)
        for c in range(4):
            for i in range(TC):
                b0 = rotK[0] % 4
                rotK[0] += 1
                pe([tr(ps[:, b0, 128 * pl:128 * pl + 128],
                       ZB[:, pl, TC - 1 - i, 4 * c:4 * c + 4, :].rearrange("p b h -> p (b h)"), ident[:, :]) for pl in range(2)],
                   reads=cb, writes=[ps_b[b0]])
                yield
                dve(lambda e, c=c, i=i, b0=b0: e.tensor_copy(KBT[:, c, i, :, :], ps[:, b0, 0:256].rearrange("p (a n) -> p a n", a=2)),
                    reads=[ps_b[b0]], writes=kc)

    dump("lam", lam[:, :, :], cb)
    dump("CT", CT[:, :, :, :], cb)
    dump("PW", PW[:, :, :, :], cb)
    dump("Ep", Ep[:, :, :, :], cb)
    dump("Rdec", Rdec[:, :], cb)
    dump("Bbar", Bbar[:, :, :, :], cb)
    dump("Kblk", Kblk[:, :, :, :], cb, BF16)
    dump("KBT", KBT[:, :, :, :, :], cb, BF16)
    dump("W4", W4[:, :, :, :, :], cb, BF16)
    dump("Wglu", Wglu[:, :, :, :, :], cb, BF16)
    es.close()
    setup_done = [Tok(P.sem[e_], P.cnt[e_]) for e_ in ("pe", "act", "dve", "pool") if P.cnt[e_] > 0]

    def bl(name, n):
        return [Buf("%s%d" % (name, i)) for i in range(n)]

    _h1 = sb("hT1", [128, DC, T]); _h2 = sb("hT2", [128, DC, T]); _h0 = sb("hT0", [128, DC, T])
    hTs = [_h0, _h1, _h2]
    hTs_b = [bl("hT%d_" % i, DC) for i in range(3)]
    xnF = sb("xnF", [128, DC, T], BF16);    xnF_b = bl("xnF", DC)
    HT = sb("HT", [128, FC, T], BF16);      HT_b = bl("HT", FC)
    fstg = sb("fstg", [128, DC, T], BF16);  fstg_b = bl("fstg", DC)
    sqF = sb("sqF", [128, DC, T], BF16);    sqF_b = bl("sqF", DC)
    rstdF = sb("rstdF", [128, 2, T]);       rstdF_b = bl("rstdF", 2)
    sgtF = sb("sgtF", [128, 2, T]);         sgtF_b = bl("sgtF", 2)
    xnM = sb("xnM", [128, DC, T], BF16);    xnM_b = bl("xnM", DC)
    stg = sb("stg", [128, DC, T], BF16);    stg_b = bl("stg", DC)
    sqM = sb("sqM", [128, DC, T], BF16);    sqM_b = bl("sqM", DC)
    rstdM = sb("rstdM", [128, 2, T]);       rstdM_b = bl("rstdM", 2)
    sgtM = sb("sgtM", [128, 1, T]);         sgtM_b = bl("sgtM", 1)
    xn2 = sb("xn2", [128, DC, T], BF16);    xn2_b = bl("xn2", DC)
    Wst = sb("Wst", [128, 2, 4, J + 2]);   Wst_b = Buf("Wst")
    Cin = sb("Cin", [128, 2, 4, J]);       Cin_b = Buf("Cin")
    Ssb, Ssb_b = Cin, Cin_b
    tw = sb("tw", [128, 4, 4, J + 1]);     tw_b = Buf("tw")
    Xb = sb("Xb", [128, 2, 4, J], BF16);   Xb_b = Buf("Xb")
    NSLOT = 8
    wsl = sb("wsl", [128, NSLOT, 1024], BF16)
    wsl_b = bl("wsl", NSLOT)
    wsl_sem = [[P.new_sem("wsl%d" % i), 0] for i in range(NSLOT)]
    xin = sb("xin", [128, 2, 512]);   xin_b = bl("xin", 2)
    xin_sem = [[P.new_sem("xin%d" % i), 0] for i in range(2)]
    xo = sb("xo", [128, 2, D]);     xo_b = bl("xo", 2)
    xo_sem = [[P.new_sem("xo%d" % i), 0] for i in range(2)]
    for _b in (hTs_b[0] + xnF_b + HT_b + fstg_b + sqF_b + rstdF_b + sgtF_b + xnM_b + stg_b + sqM_b + rstdM_b
               + sgtM_b + xn2_b + [Wst_b, Cin_b, tw_b, Xb_b] + wsl_b + xin_b + xo_b):
        for _i, _t in enumerate(chain_done):
            _b.r["setup%d" % _i] = _t

    slot_ctr = [0]
    rotF = [0]
    rotM = [0]

    def bankF():
        b_ = 4 + (rotF[0] % 4)
        rotF[0] += 1
        return b_

    def bankM():
        b_ = rotM[0] % 4
        rotM[0] += 1
        return b_

    def stream(src_ap, nelem, cvkey):
        s_ = slot_ctr[0] % NSLOT
        slot_ctr[0] += 1
        P.dma("sp", wsl[:, s_, 0:nelem], src_ap, wsl_sem[s_], reads=[cv[cvkey][2]], writes=[wsl_b[s_]])
        return s_

    class Set:
        pass

    SF = Set(); SF.xn, SF.xn_b, SF.sq, SF.sq_b, SF.rstd, SF.rstd_b, SF.bank = xnF, xnF_b, sqF, sqF_b, rstdF, rstdF_b, bankF
    S1 = Set(); S1.xn, S1.xn_b, S1.sq, S1.sq_b, S1.rstd, S1.rstd_b, S1.bank = xnF, xnF_b, sqM, sqM_b, rstdM, rstdM_b, bankM
    S2 = Set(); S2.xn, S2.xn_b, S2.sq, S2.sq_b, S2.rstd, S2.rstd_b, S2.bank = xn2, xn2_b, sqM, sqM_b, rstdM, rstdM_b, bankM
    SM = Set(); SM.xn, SM.xn_b, SM.sq, SM.sq_b, SM.rstd, SM.rstd_b, SM.bank = xnM, xnM_b, sqM, sqM_b, rstdM, rstdM_b, bankM

    def norm_stats(S, chunks, dn, ri):
        b0 = S.bank()
        n = len(chunks)
        pe([mm(ps[:, b0, 0:T], ones_bf[:, :], S.sq[:, c, :], i == 0, i == n - 1) for i, c in enumerate(chunks)],
           reads=[S.sq_b[c] for c in chunks] + [const_b], writes=[ps_b[b0]])
        yield
        act(lambda e: e.activation(S.rstd[:, ri, :], ps[:, b0, 0:T], AF.Ln, bias=eps_t[:, 0:1], scale=1.0 / dn),
            reads=[ps_b[b0], const_b], writes=[S.rstd_b[ri]])
        act(lambda e: e.activation(S.rstd[:, ri, :], S.rstd[:, ri, :], AF.Exp, scale=-0.5), reads=[], writes=[S.rstd_b[ri]])

    KBS = int(os.environ.get("KBS", "2"))

    def pre_norm(S, hT, hT_b, goff):
        for c in range(DC):
            act(lambda e, c=c: e.activation(S.sq[:, c, :], hT[:, c, :], AF.Square), reads=[hT_b[c]], writes=[S.sq_b[c]])
            if c % 2 == 1:
                yield
        yield ("bsteps", 1)
        yield from norm_stats(S, list(range(DC)), float(D), 0)
        yield ("bsteps", KBS)
        for c in range(DC):
            dve(lambda e, c=c: e.scalar_tensor_tensor(S.xn[:, c, :], hT[:, c, :], gcol[:, goff + c:goff + c + 1],
                                                      S.rstd[:, 0, :], ALU.mult, ALU.mult),
                reads=[hT_b[c], S.rstd_b[0], const_b], writes=[S.xn_b[c]])
            if c % 2 == 1:
                yield

    def post_norm_add(S, hT, hT_b, stgX, stgX_b):
        yield from norm_stats(S, list(range(DC)), float(D), 1)
        yield
        for c in range(DC):
            dve(lambda e, c=c: e.tensor_tensor(stgX[:, c, :], stgX[:, c, :], S.rstd[:, 1, :], ALU.mult),
                reads=[S.rstd_b[1]], writes=[stgX_b[c]])
            dve(lambda e, c=c: e.tensor_tensor(hT[:, c, :], hT[:, c, :], stgX[:, c, :], ALU.add),
                reads=[stgX_b[c]], writes=[hT_b[c]])
            yield

    def evac_post(S, b0, c, goff, stgX, stgX_b):
        act(lambda e: e.activation(S.sq[:, c, :], ps[:, b0, 0:T], AF.Square), reads=[ps_b[b0]], writes=[S.sq_b[c]])
        dve(lambda e: e.tensor_scalar_mul(stgX[:, c, :], ps[:, b0, 0:T], gcol[:, goff + c:goff + c + 1]),
            reads=[ps_b[b0], const_b], writes=[stgX_b[c]])

    def ffn(f, gpre, gpost, hT, hT_b, dbgt=-1, Sin=None, mark_gu=None, mark_early=None, post=True):
        S = SF
        KF = 9
        if dbgt == 0:
            dump("rstd0", S.rstd[:, 0, :], [S.rstd_b[0]])
            dump("xn1", S.xn[:, :, :], S.xn_b, BF16)
        for fc in range(FC):
            sg = stream(scr_g[f][fc].rearrange("p a c -> p (a c)"), 1024, "g%d" % f)
            su = stream(scr_u[f][fc].rearrange("p a c -> p (a c)"), 1024, "u%d" % f)
            bg, bu = bankF(), bankF()
            wgv = wsl[:, sg, :].rearrange("p (dc c) -> p dc c", c=128)
            wuv = wsl[:, su, :].rearrange("p (dc c) -> p dc c", c=128)
            pe([mm(ps[:, bg, 0:T], wgv[:, dc, :], Sin.xn[:, dc, :], dc == 0, dc == DC - 1) for dc in range(DC)],
               reads=[wsl_b[sg]] + Sin.xn_b, writes=[ps_b[bg]])
            pe([mm(ps[:, bu, 0:T], wuv[:, dc, :], Sin.xn[:, dc, :], dc == 0, dc == DC - 1) for dc in range(DC)],
               reads=[wsl_b[su]] + Sin.xn_b, writes=[ps_b[bu]])
            si = fc % 2
            act(lambda e, bg=bg, si=si: e.activation(sgtF[:, si, :], ps[:, bg, 0:T], AF.Silu),
                reads=[ps_b[bg]], writes=[sgtF_b[si]])
            dve(lambda e, bu=bu, si=si, fc=fc: e.tensor_tensor(HT[:, fc, :], sgtF[:, si, :], ps[:, bu, 0:T], ALU.mult),
                reads=[sgtF_b[si], ps_b[bu]], writes=[HT_b[fc]])
            if fc == 4 and mark_early is not None:
                yield ("set", mark_early)
            else:
                yield
        if mark_gu is not None:
            yield ("set", mark_gu)
        for dh in range(2):
            banks = [4, 5, 6, 7]
            for fc in range(FC):
                sd = stream(scr_d[f][dh, fc], 512, "d%d" % f)
                for i in range(4):
                    pe([mm(ps[:, banks[i], 0:T], wsl[:, sd, 128 * i:128 * i + 128], HT[:, fc, :], fc == 0, fc == FC - 1)],
                       reads=[wsl_b[sd], HT_b[fc]], writes=[ps_b[banks[i]]])
                if fc % 4 == 3:
                    yield
            for i in range(4):
                evac_post(S, banks[i], 4 * dh + i, gpost, fstg, fstg_b)
            yield
        if post:
            yield from post_norm_add(S, hT, hT_b, fstg, fstg_b)

    def mixer(ti, hT, hT_b):
        S = SM
        xn, xn_b, sq, sq_b, rstd, rstd_b = S.xn, S.xn_b, S.sq, S.sq_b, S.rstd, S.rstd_b
        gl, gl_b = xnM, xnM_b
        yield from pre_norm(S, hT, hT_b, G_MPRE)
        pend = None
        for oc in range(8):
            s_ = stream(scr_in[oc].rearrange("p a c -> p (a c)"), 1024, "in")
            wv = wsl[:, s_, :].rearrange("p (dc c) -> p dc c", c=128)
            b0 = bankM()
            pe([mm(ps[:, b0, 0:T], wv[:, dc, :], xn[:, dc, :], dc == 0, dc == DC - 1) for dc in range(DC)],
               reads=[wsl_b[s_]] + xn_b, writes=[ps_b[b0]])
            if pend is not None:
                pend()
            pend = (lambda b0=b0, oc=oc: act(lambda e: e.activation(UT[:, oc, 16:16 + T], ps[:, b0, 0:T], AF.Copy),
                                           reads=[ps_b[b0]], writes=[UT_b[oc]]))
            yield
        pend()
        yield

        def s_mm(c):
            Uv = UT[:, c, 16:16 + T].rearrange("p (j i) -> p j i", i=TC)
            fns = []
            for q in range(4):
                for pl in range(2):
                    for i in range(TC):
                        kw = {"tile_position": (96, 0)} if q == 3 else {}
                        fns.append(mm(ps[:, q, 256 * pl:256 * pl + J], KBT[32 * q:32 * q + 32, c, i, pl, :],
                                      Uv[32 * q:32 * q + 32, :, i], i == 0, i == TC - 1, **kw))
            pe(fns, reads=[UT_b[c], const_b, kconst_b], writes=[ps_b[q] for q in range(4)])
            yield
            act(lambda e: e.activation(Cin[:, :, :, :].rearrange("p a q j -> p q a j"),
                                       ps[:, 0:4, 0:512].rearrange("p q (a j) -> p q a j", a=2)[:, :, :, 0:J], AF.Copy),
                reads=[ps_b[q] for q in range(4)], writes=[Cin_b])

        def pool_g(g, w):
            oc = 4 + g
            b0 = bankM()
            pe([mm(ps[:, b0, 0:T], Wpool[:, g, 0 if k == 0 else 1, :], UT[:, oc, 16 - k:16 - k + T], k == 0, k == w - 1)
                for k in range(w)], reads=[UT_b[oc], const_b], writes=[ps_b[b0]])
            yield
            act(lambda e: e.activation(stg[:, oc, :], ps[:, b0, 0:T], AF.Copy,
                                       scale=gcol[:, G_PSCALE + oc - 4:G_PSCALE + oc - 3]),
                reads=[ps_b[b0], const_b], writes=[stg_b[oc]])
            act(lambda e: e.activation(sq[:, oc, :], stg[:, oc, :], AF.Square), reads=[stg_b[oc]], writes=[sq_b[oc]])
            dve(lambda e: e.tensor_copy(UT[:, oc, 0:16], UT[:, oc, T:T + 16]), reads=[], writes=[UT_b[oc]])

        def ssm_chunk(c):
            Uv = UT[:, c, 16:16 + T].rearrange("p (j i) -> p j i", i=TC)
            yield from s_mm(c)
            yield
            Sre = Cin[:, 0, :, :]
            Sim = Cin[:, 1, :, :]
            blks = slice(4 * c, 4 * c + 4)
            Epr1, Epi1 = Ep[:, 0, blks, 1:J + 1], Ep[:, 1, blks, 1:J + 1]
            sread = [Cin_b, const_b]
            dve(lambda e: e.tensor_tensor(tw[:, 0, :, 0:J], Sre, Epr1, ALU.mult), reads=sread, writes=[tw_b])
            dve(lambda e: e.tensor_tensor(tw[:, 1, :, 0:J], Sim, Epi1, ALU.mult), reads=sread, writes=[tw_b])
            yield
            dve(lambda e: e.tensor_tensor(tw[:, 2, :, 0:J], Sim, Epr1, ALU.mult), reads=sread, writes=[tw_b])
            dve(lambda e: e.tensor_tensor(tw[:, 3, :, 0:J], Sre, Epi1, ALU.mult), reads=sread, writes=[tw_b])
            yield
            dve(lambda e: e.tensor_tensor(Cin[:, 0, :, :], tw[:, 0, :, 0:J], tw[:, 1, :, 0:J], ALU.add), reads=[tw_b], writes=[Cin_b])
            dve(lambda e: e.tensor_tensor(Cin[:, 1, :, :], tw[:, 2, :, 0:J], tw[:, 3, :, 0:J], ALU.subtract), reads=[tw_b], writes=[Cin_b])
            dve(lambda e: e.tensor_copy(Wst[:, :, :, 0], carry[:, :, blks]), reads=[carry_b[c]], writes=[Wst_b])
            yield
            for pl in range(2):
                for q in range(4):
                    blk = 4 * c + q
                    dve(lambda e, pl=pl, q=q, blk=blk: e.tensor_tensor_scan(
                        Wst[:, pl, q, 1:J + 1], Rdec[:, blk:blk + 1].to_broadcast([128, J]), Cin[:, pl, q, :],
                        Wst[:, pl, q, 0:1], ALU.mult, ALU.add), reads=[Cin_b, const_b], writes=[Wst_b])
                    if q % 2 == 1:
                        yield
            Epr0, Epi0 = Ep[:, 0, blks, 0:J + 1], Ep[:, 1, blks, 0:J + 1]
            Wre, Wim = Wst[:, 0, :, 0:J + 1], Wst[:, 1, :, 0:J + 1]
            dve(lambda e: e.tensor_tensor(tw[:, 0, :, :], Wre, Epr0, ALU.mult), reads=[Wst_b, const_b], writes=[tw_b])
            dve(lambda e: e.tensor_tensor(tw[:, 1, :, :], Wim, Epi0, ALU.mult), reads=[Wst_b, const_b], writes=[tw_b])
            yield
            dve(lambda e: e.tensor_tensor(tw[:, 2, :, :], Wim, Epr0, ALU.mult), reads=[Wst_b, const_b], writes=[tw_b])
            dve(lambda e: e.tensor_tensor(tw[:, 3, :, :], Wre, Epi0, ALU.mult), reads=[Wst_b, const_b], writes=[tw_b])
            yield
            dve(lambda e: e.tensor_tensor(Xb[:, 0, :, :], tw[:, 0, :, 0:J], tw[:, 1, :, 0:J], ALU.subtract), reads=[tw_b], writes=[Xb_b])
            dve(lambda e: e.tensor_tensor(Xb[:, 1, :, :], tw[:, 2, :, 0:J], tw[:, 3, :, 0:J], ALU.add), reads=[tw_b], writes=[Xb_b])
            yield
            dve(lambda e: e.tensor_tensor(carry[:, 0, blks], tw[:, 0, :, J], tw[:, 1, :, J], ALU.subtract), reads=[tw_b], writes=[carry_b[c]])
            dve(lambda e: e.tensor_tensor(carry[:, 1, blks], tw[:, 2, :, J], tw[:, 3, :, J], ALU.add), reads=[tw_b], writes=[carry_b[c]])
            yield ("bsteps", KBS)
            by = bankM()
            Yv = ps[:, by, 0:T].rearrange("p (j i) -> p j i", i=TC)
            fns = []
            for k in range(TC):
                fns.append(mm(Yv[:, :, k:TC], Kblk[:, c, k, :], Uv[:, :, 0:TC - k], k == 0, False, skip_group_check=True))
            for q in range(4):
                blk = 4 * c + q
                for i in range(TC):
                    for pl in range(2):
                        last = (q == 3 and i == TC - 1 and pl == 1)
                        kw = {"tile_position": (0, 96)} if q == 3 else {}
                        fns.append(mm(Yv[32 * q:32 * q + 32, :, i], W4[:, blk, i, pl, :], Xb[:, pl, q, :], False, last,
                                      skip_group_check=True, **kw))
            pe(fns, reads=[UT_b[c], Xb_b, const_b, kconst_b], writes=[ps_b[by]])
            yield
            act(lambda e: e.activation(gl[:, c, :], ps[:, by, 0:T], AF.Gelu_apprx_tanh), reads=[ps_b[by]], writes=[gl_b[c]])
            yield
            bv, bg = bankM(), bankM()
            pe([mm(ps[:, bv, 0:T], Wglu[:, c, 0, :, :].rearrange("p g k -> p (g k)"), gl[:, c, :], True, True)],
               reads=[gl_b[c], const_b], writes=[ps_b[bv]])
            pe([mm(ps[:, bg, 0:T], Wglu[:, c, 1, :, :].rearrange("p g k -> p (g k)"), gl[:, c, :], True, True)],
               reads=[gl_b[c], const_b], writes=[ps_b[bg]])
            yield
            si = 0
            act(lambda e: e.activation(sgtM[:, si, :], ps[:, bg, 0:T], AF.Tanh, scale=0.5), reads=[ps_b[bg]], writes=[sgtM_b[si]])
            yield
            dve(lambda e: e.scalar_tensor_tensor(stg[:, c, :], sgtM[:, si, :], 1.0, ps[:, bv, 0:T], ALU.add, ALU.mult),
                reads=[sgtM_b[si], ps_b[bv]], writes=[stg_b[c]])
            act(lambda e: e.activation(sq[:, c, :], stg[:, c, :], AF.Square), reads=[stg_b[c]], writes=[sq_b[c]])
            yield

        for c_ in range(4):
            yield from ssm_chunk(c_)
            yield from pool_g(c_, (2, 4, 8, 16)[c_])
            yield
        if ti == 0:
            dump("UT", UT[:, :, :], UT_b, BF16)
            dump("ymix", stg[:, :, :], stg_b)
        yield from norm_stats(S, [0, 1, 2, 3], 512.0, 0)
        yield from norm_stats(S, [4, 5, 6, 7], 512.0, 1)
        yield
        for c in range(DC):
            ri = 0 if c < 4 else 1
            go = (G_SSMN + c) if c < 4 else (G_POOLN + c - 4)
            dve(lambda e, c=c, ri=ri, go=go: e.scalar_tensor_tensor(xn[:, c, :], stg[:, c, :], gcol[:, go:go + 1],
                                                                    rstd[:, ri, :], ALU.mult, ALU.mult),
                reads=[stg_b[c], rstd_b[ri], const_b], writes=[xn_b[c]])
            if c % 2 == 1:
                yield
        pend = None
        for oc in range(8):
            s_ = stream(scr_out[oc].rearrange("p a c -> p (a c)"), 1024, "out")
            wv = wsl[:, s_, :].rearrange("p (dc c) -> p dc c", c=128)
            b0 = bankM()
            pe([mm(ps[:, b0, 0:T], wv[:, dc, :], xn[:, dc, :], dc == 0, dc == DC - 1) for dc in range(DC)],
               reads=[wsl_b[s_]] + xn_b, writes=[ps_b[b0]])
            if pend is not None:
                pend()
            pend = (lambda b0=b0, oc=oc: evac_post(S, b0, oc, G_MPOST, stg, stg_b))
            yield
        pend()
        yield
        yield from post_norm_add(S, hT, hT_b, stg, stg_b)

    ld_ctr = [0]

    def load_items():
        items = []
        for bi, (a, b) in enumerate(TOKB):
            for hh in range(2):
                items.append((bi, a, b, hh))
        return items

    def load_issue(ti, k):
        bi, a, b, hh = load_items()[k]
        nb = b - a
        s_ = (ld_ctr[0] + k) % 2
        if ti == 0 and bi == 0:
            src = meta_d[0:16, 512 * hh:512 * hh + 512]
        else:
            r0 = ti * T - NMETA + a
            src = x_d[r0:r0 + nb, 512 * hh:512 * hh + 512]
        P.dma("sp", xin[0:nb, s_, :], src, xin_sem[s_], writes=[xin_b[s_]])

    def load_head(ti):
        load_issue(ti, 0)
        load_issue(ti, 1)
        yield

    def load_tile(ti, hT, hT_b, bank, head_done=False, gap=0):
        items = load_items()
        if not head_done:
            load_issue(ti, 0)
            load_issue(ti, 1)
        for k, (bi, a, b, hh) in enumerate(items):
            nb = b - a
            s_ = (ld_ctr[0] + k) % 2
            b0 = bank()
            pe([tr(ps[:, b0, 128 * i:128 * i + nb], xin[0:nb, s_, 128 * i:128 * i + 128], ident[0:nb, 0:nb])
                for i in range(4)], reads=[xin_b[s_], const_b], writes=[ps_b[b0]])
            yield
            dve(lambda e, b0=b0, hh=hh, a=a, b=b, nb=nb: e.tensor_copy(
                hT[:, 4 * hh:4 * hh + 4, a:b], ps[:, b0, :].rearrange("p (i t) -> p i t", t=128)[:, :, 0:nb]),
                reads=[ps_b[b0]], writes=[hT_b[4 * hh + i] for i in range(4)])
            if k + 2 < len(items):
                load_issue(ti, k + 2)
            for _ in range(gap):
                yield
        ld_ctr[0] += len(items)

    def store_tile(ti, hT, hT_b, bank, gap=0):
        for bi, (a, b) in enumerate(TOKB):
            nb = b - a
            if ti == 0 and bi == 0:
                continue
            s_ = bi % 2
            bks = []
            for hh in range(2):
                b0 = bank()
                bks.append(b0)
                pe([tr(ps[0:nb, b0, 128 * i:128 * i + 128], hT[:, 4 * hh + i, a:b], ident[:, :]) for i in range(4)],
                   reads=[hT_b[4 * hh + i] for i in range(4)] + [const_b], writes=[ps_b[b0]])
            yield
            for hh in range(2):
                b0 = bks[hh]
                act(lambda e, b0=b0, hh=hh, nb=nb, s_=s_: e.activation(xo[0:nb, s_, 512 * hh:512 * hh + 512], ps[0:nb, b0, :], AF.Copy),
                    reads=[ps_b[b0]], writes=[xo_b[s_]])
            r0 = ti * T - NMETA + a
            P.dma("pool", out_d[r0:r0 + nb, :], xo[0:nb, s_, :], xo_sem[s_], reads=[xo_b[s_]])
            for _ in range(gap):
                yield

    def run(g):
        for _ in g:
            pass

    def interleave(ga, gb, na, nb_):
        marks = set()
        st = {"a": [ga, 0, True, None], "b": [gb, 0, True, None]}

        def step(k):
            g = st[k]
            if g[3] is not None:
                if g[3] in marks:
                    g[3] = None
                else:
                    return False
            try:
                r = next(g[0])
            except StopIteration:
                g[2] = False
                return True
            g[1] += 1
            if isinstance(r, tuple):
                if r[0] == "set":
                    marks.add(r[1])
                elif r[0] == "wait" and r[1] not in marks:
                    g[3] = r[1]
                elif r[0] == "bsteps" and k == "a":
                    force[0] = r[1]
            return True

        force = [0]
        while st["a"][2] or st["b"][2]:
            pick = "a" if (st["a"][2] and (not st["b"][2] or st["a"][1] * nb_ <= st["b"][1] * na)) else "b"
            if force[0] > 0 and st["b"][2] and st["b"][3] is None:
                pick = "b"
                force[0] -= 1
            if not step(pick):
                other = "b" if pick == "a" else "a"
                assert st[other][2], "interleave deadlock"
                if not step(other):
                    raise RuntimeError("interleave deadlock")
        if os.environ.get("KDBG"):
            print("interleave steps A=%d B=%d" % (st["a"][1], st["b"][1]))

    def chain(*gs):
        for g in gs:
            if isinstance(g, tuple):
                yield g
            else:
                yield from g

    def zipg(g1, g2, n1=1, n2=1):
        gens = [[g1, n1, True], [g2, n2, True]]
        while gens[0][2] or gens[1][2]:
            for g in gens:
                if not g[2]:
                    continue
                for _ in range(g[1]):
                    try:
                        r = next(g[0])
                    except StopIteration:
                        g[2] = False
                        break
                    yield r

    H = lambda t: (hTs[t % 3], hTs_b[t % 3])
    pn1 = lambda t: pre_norm(S1, *H(t), G_F1PRE)
    pn2 = lambda t: pre_norm(S2, *H(t), G_F2PRE)
    f1 = lambda t, **kw: ffn(0, G_F1PRE, G_F1POST, *H(t), Sin=S1, **kw)
    f2 = lambda t, **kw: ffn(1, G_F2PRE, G_F2POST, *H(t), Sin=S2, **kw)
    fpost = lambda t: post_norm_add(SF, *H(t), fstg, fstg_b)
    run(load_tile(0, *H(0), bankF))
    conv_ffn(1)
    run(pn1(0))
    def set_late():
        late_done = [Tok(P.sem[e_], P.cnt[e_]) for e_ in ("pe", "act", "dve", "pool") if P.cnt[e_] > 0]
        for _b in hTs_b[1] + hTs_b[2]:
            for _i, _t in enumerate(late_done):
                _b.r["late%d" % _i] = _t
        yield

    ga = [klag_gen(), set_late()]
    if ntiles > 1:
        ga += [load_tile(1, *H(1), bankM), ("wait", "f1gu"), pn1(1)]
    interleave(chain(*ga), f1(0, mark_gu="f1gu"), 54, 45)
    ga = [mixer(0, *H(0)), pn2(0)]
    if ntiles > 2:
        ga += [load_tile(2, *H(2), bankM), ("wait", "f1gu"), pn1(2)]
    if ntiles > 1:
        interleave(chain(*ga), chain(f1(1, mark_gu="f1gu")), 156, 45)
    else:
        run(chain(*ga))
    pending = None
    for t in range(ntiles):
        gb = []
        if pending is not None:
            gb.append(zipg(chain(pending, ("set", "p1done")), f2(t, mark_early="b_early", post=False), 2, 1))
        else:
            gb += [("set", "p1done"), f2(t, mark_early="b_early", post=False)]
        if t + 2 < ntiles:
            gb.append(zipg(chain(fpost(t), ("set", "f2done")), f1(t + 2, mark_gu="f1gu", mark_early="f1_early", post=False), 2, 1))
            pending = fpost(t + 2)
        else:
            gb += [fpost(t), ("set", "f2done"), ("set", "f1gu"), ("set", "f1_early")]
            pending = None
        ga = [("wait", "b_early")]
        if t + 3 < ntiles:
            ga.append(load_head(t + 3))
        if t + 1 < ntiles:
            ga.append(("wait", "p1done"))
            if os.environ.get("KNOA") != "1":
                ga += [mixer(t + 1, *H(t + 1))]
            ga += [pn2(t + 1)]
        ga += [("wait", "f1_early"), ("wait", "f2done"), store_tile(t, *H(t), bankM, gap=2)]
        if t + 3 < ntiles:
            ga += [load_tile(t + 3, *H(t + 3), bankM, head_done=True, gap=1), ("wait", "f1gu"), pn1(t + 3)]
        interleave(chain(*ga), chain(*gb), 186, 91)
    assert pending is None

    P.wait_only("pool", [t_ for b_ in xo_b for t_ in b_.r.values()])
    if debug:
        P.wait_only("sp", [Tok(dbg_slot[0], dbg_slot[1])])
    P.emit()
    return nc


_NC_CACHE = {}


def kernel(**inputs):
    if "nc" not in _NC_CACHE:
        _NC_CACHE["nc"] = build()
    nc = _NC_CACHE["nc"]
    f32 = lambda a: np.ascontiguousarray(np.asarray(a, dtype=np.float32))
    x = f32(inputs["x"])
    B = x.shape[0]
    shared = {}
    for k, v in inputs.items():
        if k == "x":
            continue
        a = f32(v)
        if k != "meta_tokens":
            a = a[0]
        shared[k] = np.ascontiguousarray(a)
    in_maps = []
    for b in range(B):
        m = dict(shared)
        m["x"] = x[b]
        in_maps.append(m)
    res = run_bass_kernel_spmd(nc, in_maps, core_ids=list(range(B)))
    out = np.stack([np.asarray(r["out"], dtype=np.float32) for r in res.results], axis=0)
    return out
```
